# Optimizing a Trainium2 kernel written in Bass

```python
import math
import jax, jax.numpy as jnp
from jax import lax
import numpy as np

D_MODEL = 1024
BATCH = 8
SEQ = 4096
DEPTH = 1

HEAD_DIM = 64
CONV_WIDTH = D_MODEL // 2
CONV_GROUPS = CONV_WIDTH // HEAD_DIM
CONV_K = 3
LRU_WIDTH = D_MODEL
LRU_HEADS = LRU_WIDTH // HEAD_DIM
LRU_CONV_K = 4
LRU_C = 8.0
MIX_WIDTH = CONV_WIDTH + LRU_WIDTH
IN_COLS = 3 * CONV_WIDTH + 2 * LRU_WIDTH
D_FF = 4 * D_MODEL
EPS = 1e-6

kernel_name = "hymba_style_shortconv_rglru_block"


def rmsnorm(x, g):
    xf = x.astype(jnp.float32)
    y = xf * lax.rsqrt(jnp.mean(xf * xf, axis=-1, keepdims=True) + EPS)
    return (y * g.astype(jnp.float32)).astype(x.dtype)


def causal_dwconv(x, w):
    k_len = w.shape[0]
    s = x.shape[1]
    xp = jnp.pad(x, ((0, 0), (k_len - 1, 0), (0, 0)))
    y = w[0] * xp[:, 0:s]
    for k in range(1, k_len):
        y = y + w[k] * xp[:, k:k + s]
    return y


def block_diag_linear(x, w, b):
    bt, s, _ = x.shape
    xh = x.reshape(bt, s, LRU_HEADS, HEAD_DIM)
    y = jnp.einsum('bshi,hij->bshj', xh, w).reshape(bt, s, LRU_WIDTH)
    return y + b


def rg_lru(x, w_a, b_a, w_x, b_x, lam):
    r = jax.nn.sigmoid(block_diag_linear(x, w_a, b_a).astype(jnp.float32))
    i = jax.nn.sigmoid(block_diag_linear(x, w_x, b_x).astype(jnp.float32))
    log_a = -LRU_C * r * jax.nn.softplus(-lam.astype(jnp.float32))
    a = jnp.exp(log_a)
    mult = jnp.sqrt(-jnp.expm1(2.0 * log_a))
    bx = mult * (i * x.astype(jnp.float32))

    def combine(lhs, rhs):
        a1, b1 = lhs
        a2, b2 = rhs
        return a1 * a2, a2 * b1 + b2

    _, h = lax.associative_scan(combine, (a, bx), axis=1)
    return h.astype(x.dtype)


def setup_inputs(seed: int = 0) -> dict:
    key = jax.random.key(seed)
    ks = jax.random.split(key, 20)
    L = DEPTH
    nrm = jax.random.normal
    x = nrm(ks[0], (BATCH, SEQ, D_MODEL), jnp.float32)
    norm_mix_g = 1.0 + 0.02 * nrm(ks[1], (L, D_MODEL), jnp.float32)
    w_in = nrm(ks[2], (L, D_MODEL, IN_COLS), jnp.float32) * D_MODEL ** -0.5
    conv_w = nrm(ks[3], (L, CONV_K, CONV_WIDTH), jnp.float32) * CONV_K ** -0.5
    rnn_conv_w = nrm(ks[4], (L, LRU_CONV_K, LRU_WIDTH), jnp.float32) * LRU_CONV_K ** -0.5
    rnn_conv_b = 0.01 * nrm(ks[5], (L, LRU_WIDTH), jnp.float32)
    w_a = nrm(ks[6], (L, LRU_HEADS, HEAD_DIM, HEAD_DIM), jnp.float32) * HEAD_DIM ** -0.5
    b_a = 0.01 * nrm(ks[7], (L, LRU_WIDTH), jnp.float32)
    w_x = nrm(ks[8], (L, LRU_HEADS, HEAD_DIM, HEAD_DIM), jnp.float32) * HEAD_DIM ** -0.5
    b_x = 0.01 * nrm(ks[9], (L, LRU_WIDTH), jnp.float32)
    a_c = jax.random.uniform(ks[10], (L, LRU_WIDTH), jnp.float32, 0.9, 0.999)
    s = a_c ** (1.0 / LRU_C)
    lru_lambda = jnp.log(s) - jnp.log1p(-s)
    g_norm_conv = 1.0 + 0.02 * nrm(ks[11], (L, CONV_WIDTH), jnp.float32)
    g_norm_rnn = 1.0 + 0.02 * nrm(ks[12], (L, LRU_WIDTH), jnp.float32)
    w_out = nrm(ks[13], (L, MIX_WIDTH, D_MODEL), jnp.float32) * MIX_WIDTH ** -0.5
    norm_mlp_g = 1.0 + 0.02 * nrm(ks[14], (L, D_MODEL), jnp.float32)
    w_mlp_in = nrm(ks[15], (L, D_MODEL, D_FF), jnp.float32) * D_MODEL ** -0.5
    w_mlp_out = nrm(ks[16], (L, D_FF, D_MODEL), jnp.float32) * D_FF ** -0.5
    final_norm_g = 1.0 + 0.02 * nrm(ks[17], (D_MODEL,), jnp.float32)
    return {"x": x, "norm_mix_g": norm_mix_g, "w_in": w_in, "conv_w": conv_w,
            "rnn_conv_w": rnn_conv_w, "rnn_conv_b": rnn_conv_b,
            "w_a": w_a, "b_a": b_a, "w_x": w_x, "b_x": b_x,
            "lru_lambda": lru_lambda, "g_norm_conv": g_norm_conv,
            "g_norm_rnn": g_norm_rnn, "w_out": w_out, "norm_mlp_g": norm_mlp_g,
            "w_mlp_in": w_mlp_in, "w_mlp_out": w_mlp_out,
            "final_norm_g": final_norm_g}


def reference(x, norm_mix_g, w_in, conv_w, rnn_conv_w, rnn_conv_b, w_a, b_a,
              w_x, b_x, lru_lambda, g_norm_conv, g_norm_rnn, w_out,
              norm_mlp_g, w_mlp_in, w_mlp_out, final_norm_g):
    split_pts = [CONV_WIDTH, 2 * CONV_WIDTH, 3 * CONV_WIDTH,
                 3 * CONV_WIDTH + LRU_WIDTH]
    for l in range(DEPTH):
        h = rmsnorm(x, norm_mix_g[l])
        u = jnp.einsum('bsd,dc->bsc', h, w_in[l])
        gate_b, gate_c, v, x_r, g = jnp.split(u, split_pts, axis=-1)
        y_conv = gate_b * causal_dwconv(gate_c * v, conv_w[l])
        xr = causal_dwconv(x_r, rnn_conv_w[l]) + rnn_conv_b[l]
        y_rnn = rg_lru(xr, w_a[l], b_a[l], w_x[l], b_x[l], lru_lambda[l])
        y_rnn = y_rnn * jax.nn.gelu(g)
        y = jnp.concatenate([rmsnorm(y_conv, g_norm_conv[l]),
                             rmsnorm(y_rnn, g_norm_rnn[l])], axis=-1)
        x = x + jnp.einsum('bsc,cd->bsd', y, w_out[l])
        h = rmsnorm(x, norm_mlp_g[l])
        z = jnp.square(jax.nn.relu(jnp.einsum('bsd,df->bsf', h, w_mlp_in[l])))
        x = x + jnp.einsum('bsf,fd->bsd', z, w_mlp_out[l])
    return rmsnorm(x, final_norm_g)
```

```python
import os
import contextlib
import numpy as np
import concourse.bass as bass
import concourse.mybir as mybir
from concourse.bass_utils import run_bass_kernel_spmd

F32 = mybir.dt.float32
BF16 = mybir.dt.bfloat16
AF = mybir.ActivationFunctionType
ALU = mybir.AluOpType

ENGINES = ("pe", "act", "dve", "pool", "sp")

SEQ = 4096
D = 1024
T = 512
NCH_FULL = SEQ // T
EPS = 1e-6
NSLOT = 2
NRING = 6
NA_SLOT = 1
POOL_FREE_CHUNKS = 2
MLP_BANKS = (5, 6, 7)
WIN_BANKS = (0, 1)
MLP_SEQ = 0
FILL_DVE = 0
W2_FG = True
MLP_RATE = 0.77
SCHED_JITTER = 0.0
SCHED_SEED = 0
YSQ_ENG = 0
XRB_ENG = 0
YBF_ENG = 0
XREV_ENG = 1
PRIO_PENALTY = 0.0
STRICT = True
MIX_STEPS = 2
MLP_STEPS = 3
CAST_AHEAD = 6
CAST_DEPTH = 100

ORDER28 = list(range(20, 28)) + list(range(12, 20))
for _j in range(4):
    ORDER28 += [4 + _j, 8 + _j, _j]

PV_CW = 0
PV_RW = 12
PV_RB = 44
PV_BA = 52
PV_BX = 60
PV_LAM = 68
PV_GC = 76
PV_GR = 80
NPV = 88


class Buf:
    __slots__ = ("name", "w", "r")

    def __init__(self, name):
        self.name = name
        self.w = None
        self.r = []


class Sched:
    def __init__(self):
        self.lists = {e: [] for e in ENGINES}
        self.count = {e: 0 for e in ENGINES}
        self.seen = {e: {} for e in ENGINES}
        self.dma_sems = {}
        self.sem_names = list(ENGINES)

    def dma_sem(self, name):
        if name not in self.dma_sems:
            self.dma_sems[name] = 0
            self.sem_names.append(name)
        return name

    def _need(self, eng, ticket, waits):
        if ticket is None:
            return
        owner, cnt = ticket
        if owner == eng and cnt > self.count[eng]:
            return
        if self.seen[eng].get(owner, 0) >= cnt:
            return
        self.seen[eng][owner] = cnt
        waits.append((owner, cnt))

    def op(self, eng, fn, reads=(), writes=(), signal=True, dma=None):
        waits = []
        for b in reads:
            self._need(eng, b.w, waits)
        for b in writes:
            if b.w is not None and (STRICT or b.w[0] != eng or dma is not None):
                self._need(eng, b.w, waits)
            for t in b.r:
                if STRICT or t[0] != eng or dma is not None:
                    self._need(eng, t, waits)
        if dma is not None:
            self.dma_sems[dma] += 1
            ticket = (dma, self.dma_sems[dma])
        else:
            ticket = (eng, self.count[eng] + 1)
            if signal:
                self.count[eng] += 1
        for b in reads:
            b.r.append(ticket)
        for b in writes:
            b.w = ticket
            b.r = []
        self.lists[eng].append((waits, fn, signal, dma))
        return ticket

    def wait_all(self, eng, tickets):
        waits = []
        for t in tickets:
            self._need(eng, t, waits)
        self.lists[eng].append((waits, None, False, None))

    def emit(self, eng, handle, sems):
        for waits, fn, signal, dma in self.lists[eng]:
            for owner, cnt in waits:
                mult = 16 if owner in self.dma_sems else 1
                handle.wait_ge(sems[owner], cnt * mult)
            if fn is None:
                continue
            ins = fn(handle)
            if dma is not None:
                ins.then_inc(sems[dma], 16)
            elif signal:
                ins.then_inc(sems[eng], 1)


def _free(ap):
    n = 1
    for d in ap.shape[1:]:
        n *= int(d)
    return n


def _meta(fn, kind, elems, tset=None):
    fn.meta = (kind, elems, tset)
    return fn


def f_act(out, in_, func, bias=None, scale=None, accum=None):
    kw = {}
    if bias is not None:
        kw["bias"] = bias
    if scale is not None:
        kw["scale"] = scale
    if accum is not None:
        kw["accum_out"] = accum
    tset = "g" if func == AF.Gelu_apprx_tanh else ("e" if func in (AF.Exp, AF.Ln) else None)
    return _meta(lambda e: e.activation(out=out, in_=in_, func=func, **kw), "act", _free(in_), tset)


def f_mm(out, lhsT, rhs, start, stop):
    return _meta(lambda e: e.matmul(out, lhsT=lhsT, rhs=rhs, start=start, stop=stop), "mm", _free(rhs))


def f_tr(out, in_, ident):
    return _meta(lambda e: e.transpose(out, in_, ident), "mm", 128)


def f_tt(out, in0, in1, op):
    return _meta(lambda e: e.tensor_tensor(out=out, in0=in0, in1=in1, op=op), "tt", _free(out))


def f_ts(out, in0, s1, s2, op0, op1=None):
    if op1 is None:
        return _meta(lambda e: e.tensor_scalar(out=out, in0=in0, scalar1=s1, scalar2=None, op0=op0), "ts", _free(out))
    return _meta(lambda e: e.tensor_scalar(out=out, in0=in0, scalar1=s1, scalar2=s2, op0=op0, op1=op1), "ts", _free(out))


def f_stt(out, in0, scalar, in1, op0, op1):
    return _meta(lambda e: e.scalar_tensor_tensor(out=out, in0=in0, scalar=scalar, in1=in1, op0=op0, op1=op1), "stt", _free(out))


def f_scan(out, d0, d1, init):
    return _meta(lambda e: e.tensor_tensor_scan(out=out, data0=d0, data1=d1, initial=init, op0=ALU.mult, op1=ALU.add),
                 "scan", _free(out))


def f_copy(out, in_):
    return _meta(lambda e: e.tensor_copy(out=out, in_=in_), "ts", _free(out))


def f_memset(ap, v):
    return _meta(lambda e: e.memset(ap, v), "ts", _free(ap))


def f_dma(out, in_):
    nbytes = 1
    for d in out.shape:
        nbytes *= int(d)
    nbytes *= 2 if out.dtype == BF16 else 4
    return _meta(lambda e: e.dma_start(out=out, in_=in_), "dma", nbytes)


def op_cost(eng, fn):
    kind, n, _ = getattr(fn, "meta", ("misc", 64, None))
    if kind == "mm":
        return max(0.06, 0.005 + n / 2330.0)
    if kind == "act":
        return 0.2 + n / 1250.0
    if kind == "dma":
        return 2.0 + n / 220e3 if eng == "sp" else 3.0 + n / 90e3
    if eng == "pool":
        return (0.3 + n / 560.0) if kind == "tt" else (0.25 + n / 300.0 if n >= 256 else 0.37)
    if kind == "stt":
        return 0.17 + n / 680.0
    if kind == "tt":
        return 0.17 + n / 850.0
    if kind == "ts":
        return 0.25 + n / 950.0
    if kind == "scan":
        return 0.17 + n / 470.0
    return 0.3


class GraphSched:
    def __init__(self):
        self.ops = []
        self.sem_list = []
        self.final = None
        self.cur_thread = None
        self.thread = []

    def dma_sem(self, name):
        if name not in self.sem_list:
            self.sem_list.append(name)
        return name

    def op(self, eng, fn, reads=(), writes=(), signal=True, dma=None, tag=None):
        self.ops.append((eng, fn, tuple(reads), tuple(writes), signal, dma))
        self.thread.append(self.cur_thread)
        return len(self.ops) - 1

    def wait_all(self, eng, tickets):
        self.final = (eng, list(tickets))

    def schedule(self):
        ops = self.ops
        n = len(ops)
        unit_of = [0] * n
        units = []
        open_unit = {}
        for i, (eng, fn, rd, wr, signal, dma) in enumerate(ops):
            if dma is None and eng in open_unit:
                u = open_unit[eng]
            else:
                u = len(units)
                units.append([])
                if dma is None and not signal:
                    open_unit[eng] = u
            units[u].append(i)
            unit_of[i] = u
            if dma is None and signal and eng in open_unit:
                del open_unit[eng]
        nu = len(units)
        ueng = [ops[u[0]][0] for u in units]
        udma = [ops[u[0]][5] is not None for u in units]
        ucost = [sum(op_cost(ops[i][0], ops[i][1]) for i in u) for u in units]
        utset = [getattr(ops[u[0]][1], "meta", (0, 0, None))[2] for u in units]
        upen = [(PRIO_PENALTY if self.thread[u[0]] == 1 else 0.0) for u in units]
        lastw, readers = {}, {}
        deps = [set() for _ in range(nu)]
        for i, (eng, fn, rd, wr, signal, dma) in enumerate(ops):
            u = unit_of[i]
            for b in rd:
                w = lastw.get(id(b))
                if w is not None and w != u:
                    deps[u].add(w)
            for b in wr:
                w = lastw.get(id(b))
                if w is not None and w != u:
                    deps[u].add(w)
                for r in readers.get(id(b), ()):
                    if r != u:
                        deps[u].add(r)
            for b in rd:
                readers.setdefault(id(b), []).append(u)
            for b in wr:
                lastw[id(b)] = u
                readers[id(b)] = []
        succ = [[] for _ in range(nu)]
        indeg = [0] * nu
        for u in range(nu):
            indeg[u] = len(deps[u])
            for d in deps[u]:
                succ[d].append(u)
        if SCHED_JITTER > 0:
            _rng = np.random.RandomState(SCHED_SEED)
            jit = (_rng.rand(nu) * SCHED_JITTER).tolist()
        else:
            jit = [0.0] * nu
        eng_free = {e: 0.0 for e in ENGINES}
        cur_tset = [None]
        ready_t = [0.0] * nu
        finish = [0.0] * nu
        start = [0.0] * nu
        ready = {e: [] for e in ENGINES}
        for u in range(nu):
            if indeg[u] == 0:
                ready[ueng[u]].append(u)
        done = 0
        order = []
        WINDOW = 48
        crit = [-1] * nu
        prev_on_eng = [-1] * nu
        last_on_eng = {e: -1 for e in ENGINES}
        while done < nu:
            best = None
            for e in ENGINES:
                lst = ready[e]
                if not lst:
                    continue
                lst.sort()
                for u in lst[:WINDOW]:
                    st = max(ready_t[u], eng_free[e])
                    if e == "act" and utset[u] is not None and utset[u] != cur_tset[0]:
                        st += 2.6
                    key = (st + upen[u] + jit[u], u)
                    if best is None or key < best[0]:
                        best = (key, u, e, st)
            _, u, e, st = best
            ready[e].remove(u)
            start[u] = st
            prev_on_eng[u] = last_on_eng[e]
            last_on_eng[e] = u
            if udma[u]:
                eng_free[e] = st + (0.12 if e == "sp" else 1.2)
                finish[u] = st + ucost[u]
            else:
                if e == "act" and utset[u] is not None:
                    cur_tset[0] = utset[u]
                eng_free[e] = st + ucost[u]
                finish[u] = st + ucost[u]
            order.append(u)
            done += 1
            for v in succ[u]:
                lat = 0.6 if udma[u] else (0.08 if ueng[v] == e else 0.22)
                if finish[u] + lat > ready_t[v]:
                    ready_t[v] = finish[u] + lat
                    crit[v] = u
                indeg[v] -= 1
                if indeg[v] == 0:
                    ready[ueng[v]].append(v)
        self.sim_time = max(finish)
        if os.environ.get("MK_CRIT"):
            u = max(range(nu), key=lambda k: finish[k])
            acc = {}
            path = []
            while u >= 0:
                if start[u] > ready_t[u] + 1e-6 and prev_on_eng[u] >= 0:
                    nxt, why = prev_on_eng[u], "eng"
                else:
                    nxt, why = crit[u], "dep"
                kind = getattr(ops[units[u][0]][1], "meta", ("misc", 0, None))[0]
                key = (ueng[u], kind, why, self.thread[units[u][0]])
                seg = finish[u] - (finish[nxt] if nxt >= 0 else 0.0)
                acc[key] = acc.get(key, 0.0) + seg
                path.append((round(start[u], 1), ueng[u], kind, why, self.thread[units[u][0]]))
                u = nxt
            for k, v in sorted(acc.items(), key=lambda kv: -kv[1]):
                print("CRIT", k, round(v, 1))
            self.crit_path = path[::-1]
        self.sim_busy = {e: sum(ucost[u] if not udma[u] else 0.0 for u in range(nu) if ueng[u] == e) for e in ENGINES}
        S = Sched()
        for name in self.sem_list:
            S.dma_sem(name)
        tick = {}
        for u in order:
            for i in units[u]:
                eng, fn, rd, wr, signal, dma = ops[i]
                tick[i] = S.op(eng, fn, reads=rd, writes=wr, signal=signal, dma=dma)
        if self.final is not None:
            S.wait_all(self.final[0], [tick[i] for i in self.final[1]])
        return S


class RingMgr:
    def __init__(self, nslots, dry, order, emit_load):
        self.dry = dry
        self.order = order
        self.free = list(range(nslots))
        self.next_load = 0
        self.slot_of = {}
        self.n_use = 0
        self.emit_load = emit_load

    def use(self, blk):
        i = self.n_use
        self.n_use += 1
        if self.dry:
            self.order.append(blk)
            return (i, 0)
        assert self.order[i] == blk, (i, self.order[i], blk)
        self._pump()
        assert self.next_load > i, "ring deadlock: block %d not loadable" % i
        return (i, self.slot_of[i])

    def release(self, u):
        if self.dry:
            return
        self.free.append(self.slot_of.pop(u[0]))
        self._pump()

    def _pump(self):
        while self.next_load < len(self.order) and self.free:
            slot = self.free.pop(0)
            L = self.next_load
            self.slot_of[L] = slot
            self.emit_load(self.order[L], slot)
            self.next_load += 1


NBLK = 26


def blk_win(b):
    return b


def blk_wout(i):
    return 7 + i


def blk_w1(j):
    return 10 + 2 * j


def blk_w2(j):
    return 11 + 2 * j


def build(NCH=NCH_FULL):
    nc = bass.Bass("TRN2", target_bir_lowering=False)
    x_d = nc.dram_tensor("x", [SEQ, D], F32, kind="ExternalInput").ap()
    win_d = nc.dram_tensor("w_in", [D, 3584], F32, kind="ExternalInput").ap()
    wout_d = nc.dram_tensor("w_out", [1536, D], F32, kind="ExternalInput").ap()
    w1_d = nc.dram_tensor("w1", [D, 4096], F32, kind="ExternalInput").ap()
    w2_d = nc.dram_tensor("w2", [4096, D], F32, kind="ExternalInput").ap()
    wa_d = nc.dram_tensor("w_a", [16, 64, 64], F32, kind="ExternalInput").ap()
    wx_d = nc.dram_tensor("w_x", [16, 64, 64], F32, kind="ExternalInput").ap()
    pv_d = nc.dram_tensor("pv", [128, NPV], F32, kind="ExternalInput").ap()
    gmix_d = nc.dram_tensor("g_mix", [D], F32, kind="ExternalInput").ap()
    gmlp_d = nc.dram_tensor("g_mlp", [D], F32, kind="ExternalInput").ap()
    gfin_d = nc.dram_tensor("g_fin", [D], F32, kind="ExternalInput").ap()
    y_d = nc.dram_tensor("y", [SEQ, D], F32, kind="ExternalOutput").ap()
    ws_d = nc.dram_tensor("ws", [NBLK, 128, 4096], BF16, kind="Internal").ap()

    es = contextlib.ExitStack()
    with es:
        def sb(name, shape, dt):
            return es.enter_context(nc.sbuf_tensor("s_" + name, shape, dt))

        ring = [sb(f"ring{i}", [128, 4096], BF16) for i in range(NRING)]
        X = [[sb(f"x{b}_{m}", [128, D], F32) for m in range(4)] for b in range(2)]
        zb = [sb(f"z{i}", [128, 8, T], BF16) for i in range(2)] if W2_FG else [sb(f"z{i}", [128, 4, T], BF16) for i in range(3)]
        hTa = sb("hTa", [128, 8, T], BF16)
        hTb = sb("hTb", [128, 8, T], BF16)
        ybf = sb("ybf", [128, 12, T], BF16)
        gl = sb("gl", [128, 8, T], F32)
        gbc = [sb(f"gbc{i}", [128, D], F32) for i in range(3)]
        hn = [sb(f"hn{i}", [128, D], BF16) for i in range(2)]
        junk = sb("junk", [128, D], BF16)
        TNAMES = ("xh", "acc", "tr", "a", "ti")
        TT = [{n: sb(f"t{s}_{n}", [128, 516], F32) for n in TNAMES} for s in range(NSLOT)]
        xrb = [sb(f"xrb{s}", [128, T], BF16) for s in range(NSLOT)]
        TA = [{n: sb(f"ta{s}_{n}", [128, 516], F32) for n in ("xh", "acc", "tr")} for s in range(NA_SLOT)]
        ysq = [sb(f"ysq{i}", [128, T], BF16) for i in range(4)]
        rtt = [sb(f"rt{i}", [128, T], F32) for i in range(3)]
        BD = [sb("bda", [128, 8, 128], BF16), sb("bdx", [128, 8, 128], BF16)]
        pv = sb("pv", [128, NPV], F32)
        coef = sb("coef", [128, 8], F32)
        coef2 = sb("coef2", [128, 8], F32)
        ginv = sb("ginv", [128, 12], F32)
        nba = sb("nba", [128, 8], F32)
        nbx = sb("nbx", [128, 8], F32)
        tmp8 = sb("tmp8", [128, 8], F32)
        histA = sb("histA", [128, 4, 2], F32)
        histB = sb("histB", [128, 8, 3], F32)
        hstate = sb("hstate", [128, 8], F32)
        ident = sb("ident", [128, 128], BF16)
        identf = sb("identf", [128, 128], F32)
        ones = sb("ones", [128, 2], BF16)
        ss0 = sb("ss0", [128, 4], F32)
        rs0 = sb("rs0", [128, 4], F32)
        ss1 = sb("ss1", [128, 4], F32)
        rs1 = sb("rs1", [128, 4], F32)
        ss2 = sb("ss2", [128, 4], F32)
        rs2 = sb("rs2", [128, 4], F32)
        ssr = sb("ssr", [128, 8], F32)
        rsAB = sb("rsAB", [128, 8], F32)
        banks = [es.enter_context(nc.psum_tensor(f"bank{i}", [128, 512], F32)) for i in range(8)]
        TB = 4
        psT = banks[TB][:].bitcast(BF16)
        bdst = gl[:, 0:2, :].rearrange("p a (b c) -> p (a b) c", c=128)

        win_v = win_d.rearrange("(k p) c -> p k c", p=128)
        wout_v = wout_d.rearrange("(k p) c -> p k c", p=128)
        w1_v = w1_d.rearrange("(k p) c -> p k c", p=128)
        w2_v = w2_d.rearrange("(k p) c -> p k c", p=128)

        def emit_program(S, dry, order):
            B_ring = [Buf(f"ring{i}") for i in range(NRING)]
            B_X = [[Buf(f"x{b}_{m}") for m in range(4)] for b in range(2)]
            B_z = [Buf(f"z{j}") for j in range(3)]
            B_zg = [[Buf(f"zg{g}_{h}") for h in range(2)] for g in range(2)]
            B_hTa, B_hTb = Buf("hTa"), Buf("hTb")
            B_ybf = [Buf(f"ybf{j}") for j in range(12)]
            B_gl = [Buf(f"gl{j}") for j in range(8)]
            B_gbc = [Buf(f"gbc{i}") for i in range(3)]
            B_hn = [Buf(f"hn{i}") for i in range(2)]
            B_junk = Buf("junk")
            B_T = [{n: Buf(f"t{s}_{n}") for n in TNAMES} for s in range(NSLOT)]
            B_TA = [{n: Buf(f"ta{s}_{n}") for n in ("xh", "acc", "tr")} for s in range(NA_SLOT)]
            B_xrb = [Buf(f"xrb{s}") for s in range(NSLOT)]
            B_ysq = [Buf(f"ysq{i}") for i in range(4)]
            B_rt = [Buf(f"rt{i}") for i in range(3)]
            B_BD = [Buf("bda"), Buf("bdx")]
            B_pv, B_coef, B_nba, B_nbx, B_tmp8 = Buf("pv"), Buf("coef"), Buf("nba"), Buf("nbx"), Buf("tmp8")
            B_histA = [Buf(f"histA{j}") for j in range(4)]
            B_histB = [Buf(f"histB{j}") for j in range(8)]
            B_hst = [Buf(f"hst{j}") for j in range(8)]
            B_ident, B_identf, B_ones = Buf("ident"), Buf("identf"), Buf("ones")
            B_ss0, B_rs0, B_ss1, B_rs1 = Buf("ss0"), Buf("rs0"), Buf("ss1"), Buf("rs1")
            B_ss2 = [Buf(f"ss2_{m}") for m in range(4)]
            B_rs2 = [Buf(f"rs2_{m}") for m in range(4)]
            B_ssr, B_rsAB = Buf("ssr"), Buf("rsAB")
            B_bank = [Buf(f"bank{i}") for i in range(8)]
            B_ws = [Buf(f"ws{i}") for i in range(NBLK)]
            B_wsp = [[Buf(f"ws{i}_{k}") for k in range(3)] for i in range(NBLK)]
            B_bdst_a = Buf("bdst_a")
            B_ginv = Buf("ginv")

            for i in range(4):
                S.dma_sem(f"setup{i}")
            for i in range(2):
                S.dma_sem(f"bd{i}")
            for i in range(NBLK):
                S.dma_sem(f"cast{i}")
            for i in range(NRING):
                S.dma_sem(f"ring{i}")
            for b in range(2):
                for m in range(4):
                    S.dma_sem(f"xl{b}_{m}")
                    S.dma_sem(f"xs{b}_{m}")

            S.op("sp", f_dma(pv[:], pv_d), writes=[B_pv], dma="setup3")
            for i, g_d in enumerate((gmix_d, gmlp_d, gfin_d)):
                S.op("sp", f_dma(gbc[i][:], g_d.partition_broadcast(128)), writes=[B_gbc[i]], dma=f"setup{i}")
            S.op("pool", f_memset(identf[:], 0.0), writes=[B_identf])
            S.op("pool", lambda e: e.affine_select(out=identf[:], in_=identf[:], pattern=[[-1, 128]],
                                                  compare_op=ALU.not_equal, fill=1.0, base=0, channel_multiplier=1),
                 reads=[B_identf], writes=[B_identf])
            S.op("dve", f_copy(ident[:], identf[:]), reads=[B_identf], writes=[B_ident])
            S.op("pool", f_memset(ones[:], 1.0), writes=[B_ones])
            S.op("pool", f_memset(histA[:], 0.0), writes=B_histA)
            S.op("pool", f_memset(histB[:], 0.0), writes=B_histB)
            S.op("pool", f_memset(hstate[:], 0.0), writes=B_hst)
            B_bdst = [B_gl[0], B_gl[1]]
            for gi, w_d in enumerate((wa_d, wx_d)):
                S.op("pool", f_memset(bdst, 0.0), writes=B_bdst + [B_bdst_a])
                wv = w_d.rearrange("(j h) i k -> h i j k", h=2)
                S.op("sp", f_dma(bdst[0:64, :, 0:64], wv[0]), reads=B_bdst, writes=[B_bdst_a], dma=f"bd{gi}")
                S.op("sp", f_dma(bdst[64:128, :, 64:128], wv[1]), writes=B_bdst, dma=f"bd{gi}")
                S.op("dve", f_copy(BD[gi][:], bdst), reads=B_bdst + [B_bdst_a], writes=[B_BD[gi]])
            S.op("act", f_act(tmp8[:], pv[:, PV_LAM:PV_LAM + 8], AF.Exp, scale=-1.0), reads=[B_pv], writes=[B_tmp8])
            S.op("act", f_act(tmp8[:], tmp8[:], AF.Ln, bias=1.0), reads=[B_tmp8], writes=[B_tmp8])
            S.op("dve", f_ts(coef[:], tmp8[:], -8.0, None, ALU.mult), reads=[B_tmp8], writes=[B_coef])
            S.op("dve", f_ts(coef2[:], tmp8[:], -16.0, None, ALU.mult), reads=[B_tmp8], writes=[B_coef])
            S.op("dve", _meta(lambda e: e.reciprocal(out=ginv[:], in_=pv[:, PV_GC:PV_GC + 12]), "ts", 12), reads=[B_pv], writes=[B_ginv])
            S.op("dve", f_ts(ginv[:], ginv[:], 1e30, -1e30, ALU.min, ALU.max), reads=[B_ginv], writes=[B_ginv])
            S.op("dve", f_ts(nba[:], pv[:, PV_BA:PV_BA + 8], -1.0, None, ALU.mult), reads=[B_pv], writes=[B_nba])
            S.op("dve", f_ts(nbx[:], pv[:, PV_BX:PV_BX + 8], -1.0, None, ALU.mult), reads=[B_pv], writes=[B_nbx])

            def cast_block(b):
                _cast_block(b)

            def _thr(b):
                return [B_ws[b - CAST_DEPTH]] if b >= CAST_DEPTH else []

            def _cast_block(b):
                dst = ws_d[b]
                if b < 4:
                    c0 = ORDER28[4 * b] * 128
                    S.op("pool", f_dma(dst.rearrange("p (k c) -> p k c", k=8), win_v[:, :, c0:c0 + 512]),
                         reads=_thr(b), writes=[B_ws[b]], dma=f"cast{b}")
                elif b < 7:
                    dv = dst.rearrange("p (k c) -> p k c", k=8)
                    for i in range(4):
                        c0 = ORDER28[4 * b + i] * 128
                        S.op("pool", f_dma(dv[:, :, i * 128:(i + 1) * 128], win_v[:, :, c0:c0 + 128]),
                             reads=(_thr(b) if i == 0 else []), writes=([B_ws[b]] if i == 3 else [B_wsp[b][i]]), dma=f"cast{b}")
                elif b < 10:
                    k0 = 4 * (b - 7)
                    S.op("pool", f_dma(dst.rearrange("p (k c) -> p k c", k=4), wout_v[:, k0:k0 + 4, :]),
                         reads=_thr(b), writes=[B_ws[b]], dma=f"cast{b}")
                elif (b - 10) % 2 == 0:
                    j = (b - 10) // 2
                    S.op("pool", f_dma(dst.rearrange("p (k c) -> p k c", k=8), w1_v[:, :, j * 512:(j + 1) * 512]),
                         reads=_thr(b), writes=[B_ws[b]], dma=f"cast{b}")
                else:
                    j = (b - 11) // 2
                    if W2_FG:
                        fg, n = j // 2, j % 2
                        S.op("pool", f_dma(dst.rearrange("p (k c) -> p k c", k=8), w2_v[:, 8 * fg:8 * fg + 8, n * 512:(n + 1) * 512]),
                             reads=_thr(b), writes=[B_ws[b]], dma=f"cast{b}")
                    else:
                        S.op("pool", f_dma(dst.rearrange("p (k c) -> p k c", k=4), w2_v[:, 4 * j:4 * j + 4, :]),
                             reads=_thr(b), writes=[B_ws[b]], dma=f"cast{b}")

            cast_next = [0]

            def ensure_cast(upto):
                while cast_next[0] < min(upto + 1, NBLK):
                    cast_block(cast_next[0])
                    cast_next[0] += 1

            def emit_load(b, slot):
                ensure_cast(b + CAST_AHEAD)
                S.op("sp", f_dma(ring[slot][:], ws_d[b]), reads=[B_ws[b]] + (B_wsp[b] if 4 <= b < 7 else []),
                     writes=[B_ring[slot]], dma=f"ring{slot}")

            RM = RingMgr(NRING, dry, order, emit_load)

            def load_x_tile(s, m):
                xb = s % 2
                r0 = s * T + m * 128
                S.op("sp", f_dma(X[xb][m][:], x_d[r0:r0 + 128, :]), writes=[B_X[xb][m]], dma=f"xl{xb}_{m}")

            rr = {"win": 0, "wo": 0, "mlp": 0, "ysq": 0, "rt": 0}

            def rot(name, choices):
                i = choices[rr[name] % len(choices)]
                rr[name] += 1
                return i

            def norm_stats(xb, m, ss, B_ss):
                S.op("act", f_act(junk[:], X[xb][m][:], AF.Square, accum=ss[:, m:m + 1]),
                     reads=[B_X[xb][m]], writes=[B_junk, B_ss])

            def norm_and_transpose(xb, ss, rs, B_ss, B_rs, gi, hT, B_hT, stats_done=False):
                for m in range(4):
                    if not stats_done:
                        norm_stats(xb, m, ss, B_ss)
                S.op("act", f_act(ss[:], ss[:], AF.Ln, bias=EPS, scale=1.0 / D), reads=[B_ss], writes=[B_ss])
                S.op("act", f_act(rs[:], ss[:], AF.Exp, scale=-0.5), reads=[B_ss], writes=[B_rs])
                yield

                def nstt(m):
                    h = m % 2
                    S.op("dve", f_stt(hn[h][:], X[xb][m][:], rs[:, m:m + 1], gbc[gi][:], ALU.mult, ALU.mult),
                         reads=[B_X[xb][m], B_rs, B_gbc[gi]], writes=[B_hn[h]])

                nstt(0)
                yield
                for m in range(4):
                    h = m % 2
                    if m + 1 < 4:
                        nstt(m + 1)
                        yield
                    for k in range(8):
                        S.op("pe", f_tr(psT[:, k * 128:(k + 1) * 128], hn[h][:, k * 128:(k + 1) * 128], ident[:]),
                             reads=[B_hn[h], B_ident], writes=[B_bank[TB]], signal=(k == 7))
                    S.op("act", f_act(hT[:, :, m * 128:(m + 1) * 128], psT.rearrange("p (k c) -> p k c", k=8), AF.Copy),
                         reads=[B_bank[TB]], writes=[B_hT])
                    yield

            ss_pending = []

            def ss_matmuls(yi, jj, keep=2):
                ss_pending.append((yi, jj))
                ss_flush(keep)

            def ss_flush(keep):
                while len(ss_pending) > keep:
                    yi, jj = ss_pending.pop(0)
                    for m in range(4):
                        c = m * 12 + jj
                        S.op("pe", f_mm(banks[TB][:, c:c + 1], ysq[yi][:, m * 128:(m + 1) * 128], ones[:, 0:1], True, True),
                             reads=[B_ysq[yi], B_ones], writes=[B_bank[TB]], signal=(m == 3))

            def pe_(s):
                return "dve" if s < POOL_FREE_CHUNKS else "pool"

            def mix_thread(s):
                xb = s % 2
                yield from norm_and_transpose(xb, ss0, rs0, B_ss0, B_rs0, 0, hTa, B_hTa)

                cur = {"b": -1, "u": None}

                def win_tile(pos):
                    b = pos // 4
                    if b != cur["b"]:
                        if cur["u"] is not None:
                            RM.release(cur["u"])
                        cur["u"] = RM.use(blk_win(b))
                        cur["b"] = b
                    slot = cur["u"][1]
                    i = pos % 4
                    bi = rot("win", WIN_BANKS)
                    wv = ring[slot][:].rearrange("p (k c) -> p k c", k=8)
                    for k in range(8):
                        S.op("pe", f_mm(banks[bi][:], wv[:, k, i * 128:(i + 1) * 128], hTa[:, k, :], k == 0, k == 7),
                             reads=[B_ring[slot], B_hTa], writes=[B_bank[bi]], signal=(k == 7))
                    return bi

                for j in range(8):
                    bi = win_tile(j)
                    S.op("act", f_act(gl[:, j, :], banks[bi][:], AF.Gelu_apprx_tanh), reads=[B_bank[bi]], writes=[B_gl[j]])
                    yield

                def stageA(j):
                    sl = j % NSLOT
                    t, Bt = TT[sl], B_T[sl]
                    S.op(pe_(s), f_copy(t["xh"][:, 0:3], histB[:, j, :]), reads=[B_histB[j]], writes=[Bt["xh"]])
                    bi = win_tile(8 + j)
                    if XREV_ENG:
                        S.op("dve", f_copy(t["xh"][:, 3:515], banks[bi][:]), reads=[B_bank[bi]], writes=[Bt["xh"]])
                    else:
                        S.op("act", f_act(t["xh"][:, 3:515], banks[bi][:], AF.Copy), reads=[B_bank[bi]], writes=[Bt["xh"]])
                    S.op(pe_(s), f_copy(histB[:, j, :], t["xh"][:, 512:515]), reads=[Bt["xh"]], writes=[B_histB[j]])
                    rw = lambda k: pv[:, PV_RW + k * 8 + j:PV_RW + k * 8 + j + 1]
                    S.op("dve", f_ts(t["acc"][:, 0:T], t["xh"][:, 3:515], rw(3), pv[:, PV_RB + j:PV_RB + j + 1], ALU.mult, ALU.add),
                         reads=[Bt["xh"], B_pv], writes=[Bt["acc"]])
                    for k in (2, 1, 0):
                        S.op("dve", f_stt(t["acc"][:, 0:T], t["xh"][:, k:k + T], rw(k), t["acc"][:, 0:T], ALU.mult, ALU.add),
                             reads=[Bt["xh"], Bt["acc"], B_pv], writes=[Bt["acc"]])
                    if XRB_ENG:
                        S.op("pool", f_copy(xrb[sl][:], t["acc"][:, 0:T]), reads=[Bt["acc"]], writes=[B_xrb[sl]])
                    elif s < FILL_DVE:
                        S.op("dve", f_copy(xrb[sl][:], t["acc"][:, 0:T]), reads=[Bt["acc"]], writes=[B_xrb[sl]])
                    else:
                        S.op("act", f_act(xrb[sl][:], t["acc"][:, 0:T], AF.Copy), reads=[Bt["acc"]], writes=[B_xrb[sl]])

                def stageA2(j):
                    sl = j % NSLOT
                    S.op("pe", f_mm(banks[2][:], BD[0][:, j, :], xrb[sl][:], True, True), reads=[B_BD[0], B_xrb[sl]], writes=[B_bank[2]])
                    S.op("pe", f_mm(banks[3][:], BD[1][:, j, :], xrb[sl][:], True, True), reads=[B_BD[1], B_xrb[sl]], writes=[B_bank[3]])

                def stageB(j):
                    sl = j % NSLOT
                    t, Bt = TT[sl], B_T[sl]
                    tr, ti, a = t["tr"][:, 0:T], t["ti"][:, 0:T], t["a"][:, 0:T]
                    S.op("act", f_act(tr, banks[2][:], AF.Exp, bias=nba[:, j:j + 1], scale=-1.0), reads=[B_bank[2], B_nba], writes=[Bt["tr"]])
                    S.op("act", f_act(ti, banks[3][:], AF.Exp, bias=nbx[:, j:j + 1], scale=-1.0), reads=[B_bank[3], B_nbx], writes=[Bt["ti"]])
                    S.op("act", f_act(tr, tr, AF.Ln, bias=1.0), reads=[Bt["tr"]], writes=[Bt["tr"]])
                    S.op("act", f_act(ti, ti, AF.Ln, bias=1.0), reads=[Bt["ti"]], writes=[Bt["ti"]])
                    S.op("act", f_act(tr, tr, AF.Exp, scale=-1.0), reads=[Bt["tr"]], writes=[Bt["tr"]])
                    S.op("act", f_act(ti, ti, AF.Exp, scale=-1.0), reads=[Bt["ti"]], writes=[Bt["ti"]])
                    S.op("act", f_act(a, tr, AF.Exp, scale=coef[:, j:j + 1]), reads=[Bt["tr"], B_coef], writes=[Bt["a"]])
                    S.op(pe_(s), f_tt(ti, ti, t["acc"][:, 0:T], ALU.mult), reads=[Bt["ti"], Bt["acc"]], writes=[Bt["ti"]])
                    S.op("act", f_act(tr, tr, AF.Exp, scale=coef2[:, j:j + 1]), reads=[Bt["tr"], B_coef], writes=[Bt["tr"]])
                    S.op("act", f_act(tr, tr, AF.Ln, bias=1.0, scale=-1.0), reads=[Bt["tr"]], writes=[Bt["tr"]])
                    S.op("act", f_act(tr, tr, AF.Exp, scale=0.5), reads=[Bt["tr"]], writes=[Bt["tr"]])

                def stageC(j):
                    sl = j % NSLOT
                    t, Bt = TT[sl], B_T[sl]
                    tr, ti, a = t["tr"][:, 0:T], t["ti"][:, 0:T], t["a"][:, 0:T]
                    h = t["xh"][:, 0:T]
                    S.op("dve", f_tt(ti, ti, tr, ALU.mult), reads=[Bt["ti"], Bt["tr"]], writes=[Bt["ti"]])
                    S.op("dve", f_scan(h, a, ti, hstate[:, j:j + 1]), reads=[Bt["a"], Bt["ti"], B_hst[j]], writes=[Bt["xh"]])
                    S.op(pe_(s), f_copy(hstate[:, j:j + 1], t["xh"][:, T - 1:T]), reads=[Bt["xh"]], writes=[B_hst[j]])
                    S.op("dve", f_stt(ybf[:, 4 + j, :], h, pv[:, PV_GR + j:PV_GR + j + 1], gl[:, j, :], ALU.mult, ALU.mult),
                         reads=[Bt["xh"], B_gl[j], B_pv], writes=[B_ybf[4 + j]])
                    yi = rot("ysq", (0, 1, 2, 3))
                    S.op("act", f_act(ysq[yi][:], ybf[:, 4 + j, :], AF.Square, scale=ginv[:, 4 + j:5 + j]),
                         reads=[B_ybf[4 + j], B_ginv], writes=[B_ysq[yi]])
                    ss_matmuls(yi, 4 + j)

                for step in range(8 + 2):
                    if step < 8:
                        stageA(step)
                        yield
                    if 1 <= step < 9:
                        stageB(step - 1)
                        yield
                    if step < 8:
                        stageA2(step)
                        yield
                    if 2 <= step < 10:
                        stageC(step - 2)
                        yield

                for j in range(4):
                    if NA_SLOT:
                        sl = j % NA_SLOT
                        t, Bt = TA[sl], B_TA[sl]
                    else:
                        sl = j % NSLOT
                        t, Bt = TT[sl], B_T[sl]
                    gc, cvh, acc = t["tr"][:, 0:T], t["xh"], t["acc"][:, 0:T]
                    S.op(pe_(s), f_copy(cvh[:, 0:2], histA[:, j, :]), reads=[B_histA[j]], writes=[Bt["xh"]])
                    bi = win_tile(16 + 3 * j)
                    S.op("act", f_act(gc, banks[bi][:], AF.Copy), reads=[B_bank[bi]], writes=[Bt["tr"]])
                    bi = win_tile(16 + 3 * j + 1)
                    S.op("dve", f_tt(cvh[:, 2:514], banks[bi][:], gc, ALU.mult), reads=[B_bank[bi], Bt["tr"]], writes=[Bt["xh"]])
                    S.op(pe_(s), f_copy(histA[:, j, :], cvh[:, 512:514]), reads=[Bt["xh"]], writes=[B_histA[j]])
                    yield
                    cw = lambda k: pv[:, PV_CW + k * 4 + j:PV_CW + k * 4 + j + 1]
                    S.op("dve", f_ts(acc, cvh[:, 2:514], cw(2), None, ALU.mult), reads=[Bt["xh"], B_pv], writes=[Bt["acc"]])
                    for k in (1, 0):
                        S.op("dve", f_stt(acc, cvh[:, k:k + T], cw(k), acc, ALU.mult, ALU.add),
                             reads=[Bt["xh"], Bt["acc"], B_pv], writes=[Bt["acc"]])
                    yield
                    bi = win_tile(16 + 3 * j + 2)
                    S.op("dve", f_stt(ybf[:, j, :], banks[bi][:], pv[:, PV_GC + j:PV_GC + j + 1], acc, ALU.mult, ALU.mult),
                         reads=[B_bank[bi], Bt["acc"], B_pv], writes=[B_ybf[j]])
                    yi = rot("ysq", (0, 1, 2, 3))
                    S.op("act", f_act(ysq[yi][:], ybf[:, j, :], AF.Square, scale=ginv[:, j:j + 1]),
                         reads=[B_ybf[j], B_ginv], writes=[B_ysq[yi]])
                    ss_matmuls(yi, j)
                    yield
                RM.release(cur["u"])
                ss_flush(0)

                ssv = banks[TB][:, 0:48].rearrange("p (m j) -> p m j", j=12)
                S.op("dve", lambda e: e.tensor_reduce(out=ssr[:, 0:4], in_=ssv[:, :, 0:4], axis=mybir.AxisListType.X, op=ALU.add),
                     reads=[B_bank[TB]], writes=[B_ssr])
                S.op("dve", lambda e: e.tensor_reduce(out=ssr[:, 4:8], in_=ssv[:, :, 4:12], axis=mybir.AxisListType.X, op=ALU.add),
                     reads=[B_bank[TB]], writes=[B_ssr])
                S.op("act", f_act(ssr[:, 0:4], ssr[:, 0:4], AF.Ln, bias=EPS, scale=1.0 / 512), reads=[B_ssr], writes=[B_ssr])
                S.op("act", f_act(ssr[:, 4:8], ssr[:, 4:8], AF.Ln, bias=EPS, scale=1.0 / 1024), reads=[B_ssr], writes=[B_ssr])
                S.op("act", f_act(rsAB[:], ssr[:], AF.Exp, scale=-0.5), reads=[B_ssr], writes=[B_rsAB])
                wo = [RM.use(blk_wout(i)) for i in range(3)]
                for m in range(4):
                    for n in range(2):
                        pa = rot("wo", (0, 2))
                        pb = pa + 1
                        for k in range(12):
                            slot = wo[k // 4][1]
                            wv = ring[slot][:].rearrange("p (k c) -> p k c", k=4)
                            bi = pa if k < 4 else pb
                            S.op("pe", f_mm(banks[bi][:], ybf[:, k, m * 128:(m + 1) * 128], wv[:, k % 4, n * 512:(n + 1) * 512],
                                            k in (0, 4), k in (3, 11)),
                                 reads=[B_ring[slot], B_ybf[k]], writes=[B_bank[bi]], signal=(k in (3, 11)))
                        xs = X[xb][m][:, n * 512:(n + 1) * 512]
                        S.op("dve", f_stt(xs, banks[pa][:], rsAB[:, m:m + 1], xs, ALU.mult, ALU.add),
                             reads=[B_bank[pa], B_rsAB, B_X[xb][m]], writes=[B_X[xb][m]])
                        S.op("dve", f_stt(xs, banks[pb][:], rsAB[:, 4 + m:5 + m], xs, ALU.mult, ALU.add),
                             reads=[B_bank[pb], B_rsAB, B_X[xb][m]], writes=[B_X[xb][m]])
                        if n == 1:
                            norm_stats(xb, m, ss1, B_ss1)
                        yield
                for u in wo:
                    RM.release(u)
                yield from norm_and_transpose(xb, ss1, rs1, B_ss1, B_rs1, 1, hTb, B_hTb, stats_done=True)

            store_tickets = []

            def mlp_thread(s):
                xb = s % 2
                MB = MLP_BANKS

                def w1_block(j):
                    u = RM.use(blk_w1(j))
                    wv = ring[u[1]][:].rearrange("p (k c) -> p k c", k=8)
                    zi = j % 3
                    for f in range(4):
                        bi = rot("mlp", MB)
                        for k in range(8):
                            S.op("pe", f_mm(banks[bi][:], wv[:, k, f * 128:(f + 1) * 128], hTb[:, k, :], k == 0, k == 7),
                                 reads=[B_ring[u[1]], B_hTb], writes=[B_bank[bi]], signal=(k == 7))
                        ri = rot("rt", (0, 1, 2))
                        S.op("act", f_act(rtt[ri][:], banks[bi][:], AF.Relu), reads=[B_bank[bi]], writes=[B_rt[ri]])
                        if W2_FG:
                            g, h = (j // 2) % 2, j % 2
                            if s < POOL_FREE_CHUNKS:
                                S.op("act", f_act(zb[g][:, 4 * h + f, :], rtt[ri][:], AF.Square), reads=[B_rt[ri]], writes=[B_zg[g][h]])
                            else:
                                S.op("pool", f_tt(zb[g][:, 4 * h + f, :], rtt[ri][:], rtt[ri][:], ALU.mult), reads=[B_rt[ri]], writes=[B_zg[g][h]])
                        else:
                            S.op("pool", f_tt(zb[zi][:, f, :], rtt[ri][:], rtt[ri][:], ALU.mult), reads=[B_rt[ri]], writes=[B_z[zi]])
                        yield
                    RM.release(u)

                def w2_block(j):
                    u = RM.use(blk_w2(j))
                    wv = ring[u[1]][:].rearrange("p (k c) -> p k c", k=4)
                    zi = j % 3
                    for m in range(4):
                        for n in range(2):
                            bi = rot("mlp", MB)
                            for f in range(4):
                                S.op("pe", f_mm(banks[bi][:], zb[zi][:, f, m * 128:(m + 1) * 128], wv[:, f, n * 512:(n + 1) * 512],
                                                f == 0, f == 3),
                                     reads=[B_ring[u[1]], B_z[zi]], writes=[B_bank[bi]], signal=(f == 3))
                            xs = X[xb][m][:, n * 512:(n + 1) * 512]
                            S.op("dve", f_tt(xs, banks[bi][:], xs, ALU.add), reads=[B_bank[bi], B_X[xb][m]], writes=[B_X[xb][m]])
                            yield
                    RM.release(u)

                def w2_fg(fg, n):
                    u = RM.use(blk_w2(2 * fg + n))
                    wv = ring[u[1]][:].rearrange("p (k c) -> p k c", k=8)
                    g = fg % 2
                    for m in range(4):
                        bi = rot("mlp", MB)
                        for f in range(8):
                            S.op("pe", f_mm(banks[bi][:], zb[g][:, f, m * 128:(m + 1) * 128], wv[:, f, :], f == 0, f == 7),
                                 reads=[B_ring[u[1]], B_zg[g][0], B_zg[g][1]], writes=[B_bank[bi]], signal=(f == 7))
                        xs = X[xb][m][:, n * 512:(n + 1) * 512]
                        S.op("dve", f_tt(xs, banks[bi][:], xs, ALU.add), reads=[B_bank[bi], B_X[xb][m]], writes=[B_X[xb][m]])
                        yield
                    RM.release(u)

                if W2_FG:
                    if MLP_SEQ == 1:
                        seq = [("w1", 0), ("w1", 1), ("w2", 0, 0), ("w1", 2), ("w2", 0, 1), ("w1", 3), ("w2", 1, 0), ("w1", 4),
                               ("w2", 1, 1), ("w1", 5), ("w2", 2, 0), ("w1", 6), ("w2", 2, 1), ("w1", 7), ("w2", 3, 0), ("w2", 3, 1)]
                    elif MLP_SEQ == 2:
                        seq = [("w1", 0), ("w1", 1), ("w1", 2), ("w1", 3), ("w2", 0, 0), ("w2", 0, 1), ("w1", 4), ("w1", 5),
                               ("w2", 1, 0), ("w2", 1, 1), ("w1", 6), ("w1", 7), ("w2", 2, 0), ("w2", 2, 1), ("w2", 3, 0), ("w2", 3, 1)]
                    else:
                        seq = [("w1", 0), ("w1", 1), ("w1", 2), ("w2", 0, 0), ("w1", 3), ("w2", 0, 1), ("w1", 4), ("w2", 1, 0),
                               ("w1", 5), ("w2", 1, 1), ("w1", 6), ("w2", 2, 0), ("w1", 7), ("w2", 2, 1), ("w2", 3, 0), ("w2", 3, 1)]
                    for it in seq:
                        if it[0] == "w1":
                            yield from w1_block(it[1])
                        else:
                            yield from w2_fg(it[1], it[2])
                else:
                    yield from w1_block(0)
                    for j in range(1, 8):
                        yield from w1_block(j)
                        yield from w2_block(j - 1)
                    yield from w2_block(7)
                for m in range(4):
                    S.op("act", f_act(junk[:], X[xb][m][:], AF.Square, accum=ss2[:, m:m + 1]),
                         reads=[B_X[xb][m]], writes=[B_junk, B_ss2[m]])
                    S.op("act", f_act(ss2[:, m:m + 1], ss2[:, m:m + 1], AF.Ln, bias=EPS, scale=1.0 / D), reads=[B_ss2[m]], writes=[B_ss2[m]])
                    S.op("act", f_act(rs2[:, m:m + 1], ss2[:, m:m + 1], AF.Exp, scale=-0.5), reads=[B_ss2[m]], writes=[B_rs2[m]])
                    S.op("dve", f_stt(X[xb][m][:], X[xb][m][:], rs2[:, m:m + 1], gbc[2][:], ALU.mult, ALU.mult),
                         reads=[B_X[xb][m], B_rs2[m], B_gbc[2]], writes=[B_X[xb][m]])
                    r0 = s * T + m * 128
                    tk = S.op("sp", f_dma(y_d[r0:r0 + 128, :], X[xb][m][:]), reads=[B_X[xb][m]], writes=[], dma=f"xs{xb}_{m}")
                    store_tickets.append(tk)
                    if s + 2 < NCH:
                        load_x_tile(s + 2, m)
                    yield

            def drive_rate(ga, gb, rate):
                acc = 0.0
                a_alive, b_alive = ga is not None, gb is not None
                while a_alive:
                    try:
                        S.cur_thread = 0
                        next(ga)
                    except StopIteration:
                        a_alive = False
                        break
                    acc += rate
                    while b_alive and acc >= 1.0:
                        acc -= 1.0
                        try:
                            S.cur_thread = 1
                            next(gb)
                        except StopIteration:
                            b_alive = False
                while b_alive:
                    try:
                        S.cur_thread = 1
                        next(gb)
                    except StopIteration:
                        b_alive = False

            def drive(ga, gb, na, nb):
                if MLP_RATE > 0:
                    return drive_rate(ga, gb, MLP_RATE)
                alive = [ga is not None, gb is not None]
                gens = [ga, gb]
                cnt = [na, nb]
                while alive[0] or alive[1]:
                    for i in range(2):
                        if not alive[i]:
                            continue
                        for _ in range(cnt[i] if alive[1 - i] else 1000000):
                            try:
                                S.cur_thread = i
                                next(gens[i])
                            except StopIteration:
                                alive[i] = False
                                break

            for m in range(4):
                load_x_tile(0, m)
            if NCH > 1:
                for m in range(4):
                    load_x_tile(1, m)
            drive(mix_thread(0), None, 1, 1)
            for s in range(NCH):
                drive(mix_thread(s + 1) if s + 1 < NCH else None, mlp_thread(s), MIX_STEPS, MLP_STEPS)
            S.wait_all("sp", store_tickets)

        order = []
        emit_program(GraphSched(), True, order)
        G = GraphSched()
        emit_program(G, False, order)
        S = G.schedule()
        if os.environ.get("MK_VERBOSE"):
            print("list-schedule simulated time (us):", G.sim_time, "ops", len(G.ops), {k: round(v) for k, v in G.sim_busy.items()})

        with contextlib.ExitStack() as es2:
            sems = {n: es2.enter_context(nc.semaphore(n)) for n in S.sem_names}
            block = es2.enter_context(nc.Block())

            @block.sync
            def _(e):
                S.emit("sp", e, sems)

            @block.scalar
            def _(e):
                S.emit("act", e, sems)

            @block.vector
            def _(e):
                S.emit("dve", e, sems)

            @block.gpsimd
            def _(e):
                S.emit("pool", e, sems)

            @block.tensor
            def _(e):
                S.emit("pe", e, sems)
    return nc


def _pack_params(conv_w, rnn_conv_w, rnn_conv_b, b_a, b_x, lru_lambda, g_norm_conv, g_norm_rnn):
    def cols(v):
        return np.ascontiguousarray(v.reshape(-1, 128).T)
    parts = []
    parts.append(np.concatenate([cols(conv_w[0, k]) for k in range(3)], axis=1))
    parts.append(np.concatenate([cols(rnn_conv_w[0, k]) for k in range(4)], axis=1))
    for v in (rnn_conv_b[0], b_a[0], b_x[0], lru_lambda[0], g_norm_conv[0], g_norm_rnn[0]):
        parts.append(cols(v))
    pv = np.ascontiguousarray(np.concatenate(parts, axis=1).astype(np.float32))
    assert pv.shape == (128, NPV), pv.shape
    return pv


def kernel(x, norm_mix_g, w_in, conv_w, rnn_conv_w, rnn_conv_b, w_a, b_a, w_x, b_x, lru_lambda,
           g_norm_conv, g_norm_rnn, w_out, norm_mlp_g, w_mlp_in, w_mlp_out, final_norm_g):
    nch = int(os.environ.get("MK_NCH", NCH_FULL))
    f = lambda a: np.ascontiguousarray(np.asarray(a, dtype=np.float32))
    x = f(x)
    pv = _pack_params(f(conv_w), f(rnn_conv_w), f(rnn_conv_b), f(b_a), f(b_x), f(lru_lambda),
                      f(g_norm_conv), f(g_norm_rnn))
    shared = {
        "w_in": f(w_in)[0], "w_out": f(w_out)[0], "w1": f(w_mlp_in)[0], "w2": f(w_mlp_out)[0],
        "w_a": f(w_a)[0], "w_x": f(w_x)[0], "pv": pv,
        "g_mix": f(norm_mix_g)[0], "g_mlp": f(norm_mlp_g)[0], "g_fin": f(final_norm_g),
    }
    nc = build(nch)
    ncores = x.shape[0]
    in_maps = [dict(shared, x=x[c]) for c in range(ncores)]
    res = run_bass_kernel_spmd(nc, in_maps, core_ids=list(range(ncores)))
    out = np.stack([np.asarray(res.results[c]["y"], dtype=np.float32) for c in range(ncores)], axis=0)
    if int(os.environ.get("MK_DBG", "0")):
        kernel.dbg = {k: np.asarray(v) for k, v in res.results[0].items() if k.startswith("d_")}
    return out
```

```python
import os
import contextlib
import numpy as np
import concourse.bass as bass
import concourse.mybir as mybir
from concourse.bass_utils import run_bass_kernel_spmd

F32 = mybir.dt.float32
BF16 = mybir.dt.bfloat16
AF = mybir.ActivationFunctionType
ALU = mybir.AluOpType

ENGINES = ("pe", "act", "dve", "pool", "sp")

SEQ = 4096
D = 1024
T = 512
NCH_FULL = SEQ // T
EPS = 1e-6
NSLOT = 2
NRING = 6
NA_SLOT = 1
POOL_FREE_CHUNKS = 2
MLP_BANKS = (5, 6, 7)
WIN_BANKS = (0, 1)
MLP_SEQ = 0
FILL_DVE = 0
W2_FG = True
MLP_RATE = 0.77
SCHED_JITTER = 0.0
SCHED_SEED = 0
YSQ_ENG = 0
XRB_ENG = 0
YBF_ENG = 0
XREV_ENG = 1
PRIO_PENALTY = 0.0
STRICT = True
MIX_STEPS = 2
MLP_STEPS = 3
CAST_AHEAD = 6
CAST_DEPTH = 100

ORDER28 = list(range(20, 28)) + list(range(12, 20))
for _j in range(4):
    ORDER28 += [4 + _j, 8 + _j, _j]

PV_CW = 0
PV_RW = 12
PV_RB = 44
PV_BA = 52
PV_BX = 60
PV_LAM = 68
PV_GC = 76
PV_GR = 80
NPV = 88


class Buf:
    __slots__ = ("name", "w", "r")

    def __init__(self, name):
        self.name = name
        self.w = None
        self.r = []


class Sched:
    def __init__(self):
        self.lists = {e: [] for e in ENGINES}
        self.count = {e: 0 for e in ENGINES}
        self.seen = {e: {} for e in ENGINES}
        self.dma_sems = {}
        self.sem_names = list(ENGINES)

    def dma_sem(self, name):
        if name not in self.dma_sems:
            self.dma_sems[name] = 0
            self.sem_names.append(name)
        return name

    def _need(self, eng, ticket, waits):
        if ticket is None:
            return
        owner, cnt = ticket
        if owner == eng and cnt > self.count[eng]:
            return
        if self.seen[eng].get(owner, 0) >= cnt:
            return
        self.seen[eng][owner] = cnt
        waits.append((owner, cnt))

    def op(self, eng, fn, reads=(), writes=(), signal=True, dma=None):
        waits = []
        for b in reads:
            self._need(eng, b.w, waits)
        for b in writes:
            if b.w is not None and (STRICT or b.w[0] != eng or dma is not None):
                self._need(eng, b.w, waits)
            for t in b.r:
                if STRICT or t[0] != eng or dma is not None:
                    self._need(eng, t, waits)
        if dma is not None:
            self.dma_sems[dma] += 1
            ticket = (dma, self.dma_sems[dma])
        else:
            ticket = (eng, self.count[eng] + 1)
            if signal:
                self.count[eng] += 1
        for b in reads:
            b.r.append(ticket)
        for b in writes:
            b.w = ticket
            b.r = []
        self.lists[eng].append((waits, fn, signal, dma))
        return ticket

    def wait_all(self, eng, tickets):
        waits = []
        for t in tickets:
            self._need(eng, t, waits)
        self.lists[eng].append((waits, None, False, None))

    def emit(self, eng, handle, sems):
        for waits, fn, signal, dma in self.lists[eng]:
            for owner, cnt in waits:
                mult = 16 if owner in self.dma_sems else 1
                handle.wait_ge(sems[owner], cnt * mult)
            if fn is None:
                continue
            ins = fn(handle)
            if dma is not None:
                ins.then_inc(sems[dma], 16)
            elif signal:
                ins.then_inc(sems[eng], 1)


def _free(ap):
    n = 1
    for d in ap.shape[1:]:
        n *= int(d)
    return n


def _meta(fn, kind, elems, tset=None):
    fn.meta = (kind, elems, tset)
    return fn


def f_act(out, in_, func, bias=None, scale=None, accum=None):
    kw = {}
    if bias is not None:
        kw["bias"] = bias
    if scale is not None:
        kw["scale"] = scale
    if accum is not None:
        kw["accum_out"] = accum
    tset = "g" if func == AF.Gelu_apprx_tanh else ("e" if func in (AF.Exp, AF.Ln) else None)
    return _meta(lambda e: e.activation(out=out, in_=in_, func=func, **kw), "act", _free(in_), tset)


def f_mm(out, lhsT, rhs, start, stop):
    return _meta(lambda e: e.matmul(out, lhsT=lhsT, rhs=rhs, start=start, stop=stop), "mm", _free(rhs))


def f_tr(out, in_, ident):
    return _meta(lambda e: e.transpose(out, in_, ident), "mm", 128)


def f_tt(out, in0, in1, op):
    return _meta(lambda e: e.tensor_tensor(out=out, in0=in0, in1=in1, op=op), "tt", _free(out))


def f_ts(out, in0, s1, s2, op0, op1=None):
    if op1 is None:
        return _meta(lambda e: e.tensor_scalar(out=out, in0=in0, scalar1=s1, scalar2=None, op0=op0), "ts", _free(out))
    return _meta(lambda e: e.tensor_scalar(out=out, in0=in0, scalar1=s1, scalar2=s2, op0=op0, op1=op1), "ts", _free(out))


def f_stt(out, in0, scalar, in1, op0, op1):
    return _meta(lambda e: e.scalar_tensor_tensor(out=out, in0=in0, scalar=scalar, in1=in1, op0=op0, op1=op1), "stt", _free(out))


def f_scan(out, d0, d1, init):
    return _meta(lambda e: e.tensor_tensor_scan(out=out, data0=d0, data1=d1, initial=init, op0=ALU.mult, op1=ALU.add),
                 "scan", _free(out))


def f_copy(out, in_):
    return _meta(lambda e: e.tensor_copy(out=out, in_=in_), "ts", _free(out))


def f_memset(ap, v):
    return _meta(lambda e: e.memset(ap, v), "ts", _free(ap))


def f_dma(out, in_):
    nbytes = 1
    for d in out.shape:
        nbytes *= int(d)
    nbytes *= 2 if out.dtype == BF16 else 4
    return _meta(lambda e: e.dma_start(out=out, in_=in_), "dma", nbytes)


def op_cost(eng, fn):
    kind, n, _ = getattr(fn, "meta", ("misc", 64, None))
    if kind == "mm":
        return max(0.06, 0.005 + n / 2330.0)
    if kind == "act":
        return 0.2 + n / 1250.0
    if kind == "dma":
        return 2.0 + n / 220e3 if eng == "sp" else 3.0 + n / 90e3
    if eng == "pool":
        return (0.3 + n / 560.0) if kind == "tt" else (0.25 + n / 300.0 if n >= 256 else 0.37)
    if kind == "stt":
        return 0.17 + n / 680.0
    if kind == "tt":
        return 0.17 + n / 850.0
    if kind == "ts":
        return 0.25 + n / 950.0
    if kind == "scan":
        return 0.17 + n / 470.0
    return 0.3


class GraphSched:
    def __init__(self):
        self.ops = []
        self.sem_list = []
        self.final = None
        self.cur_thread = None
        self.thread = []

    def dma_sem(self, name):
        if name not in self.sem_list:
            self.sem_list.append(name)
        return name

    def op(self, eng, fn, reads=(), writes=(), signal=True, dma=None, tag=None):
        self.ops.append((eng, fn, tuple(reads), tuple(writes), signal, dma))
        self.thread.append(self.cur_thread)
        return len(self.ops) - 1

    def wait_all(self, eng, tickets):
        self.final = (eng, list(tickets))

    def schedule(self):
        ops = self.ops
        n = len(ops)
        unit_of = [0] * n
        units = []
        open_unit = {}
        for i, (eng, fn, rd, wr, signal, dma) in enumerate(ops):
            if dma is None and eng in open_unit:
                u = open_unit[eng]
            else:
                u = len(units)
                units.append([])
                if dma is None and not signal:
                    open_unit[eng] = u
            units[u].append(i)
            unit_of[i] = u
            if dma is None and signal and eng in open_unit:
                del open_unit[eng]
        nu = len(units)
        ueng = [ops[u[0]][0] for u in units]
        udma = [ops[u[0]][5] is not None for u in units]
        ucost = [sum(op_cost(ops[i][0], ops[i][1]) for i in u) for u in units]
        utset = [getattr(ops[u[0]][1], "meta", (0, 0, None))[2] for u in units]
        upen = [(PRIO_PENALTY if self.thread[u[0]] == 1 else 0.0) for u in units]
        lastw, readers = {}, {}
        deps = [set() for _ in range(nu)]
        for i, (eng, fn, rd, wr, signal, dma) in enumerate(ops):
            u = unit_of[i]
            for b in rd:
                w = lastw.get(id(b))
                if w is not None and w != u:
                    deps[u].add(w)
            for b in wr:
                w = lastw.get(id(b))
                if w is not None and w != u:
                    deps[u].add(w)
                for r in readers.get(id(b), ()):
                    if r != u:
                        deps[u].add(r)
            for b in rd:
                readers.setdefault(id(b), []).append(u)
            for b in wr:
                lastw[id(b)] = u
                readers[id(b)] = []
        succ = [[] for _ in range(nu)]
        indeg = [0] * nu
        for u in range(nu):
            indeg[u] = len(deps[u])
            for d in deps[u]:
                succ[d].append(u)
        if SCHED_JITTER > 0:
            _rng = np.random.RandomState(SCHED_SEED)
            jit = (_rng.rand(nu) * SCHED_JITTER).tolist()
        else:
            jit = [0.0] * nu
        eng_free = {e: 0.0 for e in ENGINES}
        cur_tset = [None]
        ready_t = [0.0] * nu
        finish = [0.0] * nu
        start = [0.0] * nu
        ready = {e: [] for e in ENGINES}
        for u in range(nu):
            if indeg[u] == 0:
                ready[ueng[u]].append(u)
        done = 0
        order = []
        WINDOW = 48
        crit = [-1] * nu
        prev_on_eng = [-1] * nu
        last_on_eng = {e: -1 for e in ENGINES}
        while done < nu:
            best = None
            for e in ENGINES:
                lst = ready[e]
                if not lst:
                    continue
                lst.sort()
                for u in lst[:WINDOW]:
                    st = max(ready_t[u], eng_free[e])
                    if e == "act" and utset[u] is not None and utset[u] != cur_tset[0]:
                        st += 2.6
                    key = (st + upen[u] + jit[u], u)
                    if best is None or key < best[0]:
                        best = (key, u, e, st)
            _, u, e, st = best
            ready[e].remove(u)
            start[u] = st
            prev_on_eng[u] = last_on_eng[e]
            last_on_eng[e] = u
            if udma[u]:
                eng_free[e] = st + (0.12 if e == "sp" else 1.2)
                finish[u] = st + ucost[u]
            else:
                if e == "act" and utset[u] is not None:
                    cur_tset[0] = utset[u]
                eng_free[e] = st + ucost[u]
                finish[u] = st + ucost[u]
            order.append(u)
            done += 1
            for v in succ[u]:
                lat = 0.6 if udma[u] else (0.08 if ueng[v] == e else 0.22)
                if finish[u] + lat > ready_t[v]:
                    ready_t[v] = finish[u] + lat
                    crit[v] = u
                indeg[v] -= 1
                if indeg[v] == 0:
                    ready[ueng[v]].append(v)
        self.sim_time = max(finish)
        if os.environ.get("MK_CRIT"):
            u = max(range(nu), key=lambda k: finish[k])
            acc = {}
            path = []
            while u >= 0:
                if start[u] > ready_t[u] + 1e-6 and prev_on_eng[u] >= 0:
                    nxt, why = prev_on_eng[u], "eng"
                else:
                    nxt, why = crit[u], "dep"
                kind = getattr(ops[units[u][0]][1], "meta", ("misc", 0, None))[0]
                key = (ueng[u], kind, why, self.thread[units[u][0]])
                seg = finish[u] - (finish[nxt] if nxt >= 0 else 0.0)
                acc[key] = acc.get(key, 0.0) + seg
                path.append((round(start[u], 1), ueng[u], kind, why, self.thread[units[u][0]]))
                u = nxt
            for k, v in sorted(acc.items(), key=lambda kv: -kv[1]):
                print("CRIT", k, round(v, 1))
            self.crit_path = path[::-1]
        self.sim_busy = {e: sum(ucost[u] if not udma[u] else 0.0 for u in range(nu) if ueng[u] == e) for e in ENGINES}
        S = Sched()
        for name in self.sem_list:
            S.dma_sem(name)
        tick = {}
        for u in order:
            for i in units[u]:
                eng, fn, rd, wr, signal, dma = ops[i]
                tick[i] = S.op(eng, fn, reads=rd, writes=wr, signal=signal, dma=dma)
        if self.final is not None:
            S.wait_all(self.final[0], [tick[i] for i in self.final[1]])
        return S


class RingMgr:
    def __init__(self, nslots, dry, order, emit_load):
        self.dry = dry
        self.order = order
        self.free = list(range(nslots))
        self.next_load = 0
        self.slot_of = {}
        self.n_use = 0
        self.emit_load = emit_load

    def use(self, blk):
        i = self.n_use
        self.n_use += 1
        if self.dry:
            self.order.append(blk)
            return (i, 0)
        assert self.order[i] == blk, (i, self.order[i], blk)
        self._pump()
        assert self.next_load > i, "ring deadlock: block %d not loadable" % i
        return (i, self.slot_of[i])

    def release(self, u):
        if self.dry:
            return
        self.free.append(self.slot_of.pop(u[0]))
        self._pump()

    def _pump(self):
        while self.next_load < len(self.order) and self.free:
            slot = self.free.pop(0)
            L = self.next_load
            self.slot_of[L] = slot
            self.emit_load(self.order[L], slot)
            self.next_load += 1


NBLK = 26


def blk_win(b):
    return b


def blk_wout(i):
    return 7 + i


def blk_w1(j):
    return 10 + 2 * j


def blk_w2(j):
    return 11 + 2 * j


def build(NCH=NCH_FULL):
    nc = bass.Bass("TRN2", target_bir_lowering=False)
    x_d = nc.dram_tensor("x", [SEQ, D], F32, kind="ExternalInput").ap()
    win_d = nc.dram_tensor("w_in", [D, 3584], F32, kind="ExternalInput").ap()
    wout_d = nc.dram_tensor("w_out", [1536, D], F32, kind="ExternalInput").ap()
    w1_d = nc.dram_tensor("w1", [D, 4096], F32, kind="ExternalInput").ap()
    w2_d = nc.dram_tensor("w2", [4096, D], F32, kind="ExternalInput").ap()
    wa_d = nc.dram_tensor("w_a", [16, 64, 64], F32, kind="ExternalInput").ap()
    wx_d = nc.dram_tensor("w_x", [16, 64, 64], F32, kind="ExternalInput").ap()
    pv_d = nc.dram_tensor("pv", [128, NPV], F32, kind="ExternalInput").ap()
    gmix_d = nc.dram_tensor("g_mix", [D], F32, kind="ExternalInput").ap()
    gmlp_d = nc.dram_tensor("g_mlp", [D], F32, kind="ExternalInput").ap()
    gfin_d = nc.dram_tensor("g_fin", [D], F32, kind="ExternalInput").ap()
    y_d = nc.dram_tensor("y", [SEQ, D], F32, kind="ExternalOutput").ap()
    ws_d = nc.dram_tensor("ws", [NBLK, 128, 4096], BF16, kind="Internal").ap()

    es = contextlib.ExitStack()
    with es:
        def sb(name, shape, dt):
            return es.enter_context(nc.sbuf_tensor("s_" + name, shape, dt))

        ring = [sb(f"ring{i}", [128, 4096], BF16) for i in range(NRING)]
        X = [[sb(f"x{b}_{m}", [128, D], F32) for m in range(4)] for b in range(2)]
        zb = [sb(f"z{i}", [128, 8, T], BF16) for i in range(2)] if W2_FG else [sb(f"z{i}", [128, 4, T], BF16) for i in range(3)]
        hTa = sb("hTa", [128, 8, T], BF16)
        hTb = sb("hTb", [128, 8, T], BF16)
        ybf = sb("ybf", [128, 12, T], BF16)
        gl = sb("gl", [128, 8, T], F32)
        gbc = [sb(f"gbc{i}", [128, D], F32) for i in range(3)]
        hn = [sb(f"hn{i}", [128, D], BF16) for i in range(2)]
        junk = sb("junk", [128, D], BF16)
        TNAMES = ("xh", "acc", "tr", "a", "ti")
        TT = [{n: sb(f"t{s}_{n}", [128, 516], F32) for n in TNAMES} for s in range(NSLOT)]
        xrb = [sb(f"xrb{s}", [128, T], BF16) for s in range(NSLOT)]
        TA = [{n: sb(f"ta{s}_{n}", [128, 516], F32) for n in ("xh", "acc", "tr")} for s in range(NA_SLOT)]
        ysq = [sb(f"ysq{i}", [128, T], BF16) for i in range(4)]
        rtt = [sb(f"rt{i}", [128, T], F32) for i in range(3)]
        BD = [sb("bda", [128, 8, 128], BF16), sb("bdx", [128, 8, 128], BF16)]
        pv = sb("pv", [128, NPV], F32)
        coef = sb("coef", [128, 8], F32)
        coef2 = sb("coef2", [128, 8], F32)
        ginv = sb("ginv", [128, 12], F32)
        nba = sb("nba", [128, 8], F32)
        nbx = sb("nbx", [128, 8], F32)
        tmp8 = sb("tmp8", [128, 8], F32)
        histA = sb("histA", [128, 4, 2], F32)
        histB = sb("histB", [128, 8, 3], F32)
        hstate = sb("hstate", [128, 8], F32)
        ident = sb("ident", [128, 128], BF16)
        identf = sb("identf", [128, 128], F32)
        ones = sb("ones", [128, 2], BF16)
        ss0 = sb("ss0", [128, 4], F32)
        rs0 = sb("rs0", [128, 4], F32)
        ss1 = sb("ss1", [128, 4], F32)
        rs1 = sb("rs1", [128, 4], F32)
        ss2 = sb("ss2", [128, 4], F32)
        rs2 = sb("rs2", [128, 4], F32)
        ssr = sb("ssr", [128, 8], F32)
        rsAB = sb("rsAB", [128, 8], F32)
        banks = [es.enter_context(nc.psum_tensor(f"bank{i}", [128, 512], F32)) for i in range(8)]
        TB = 4
        psT = banks[TB][:].bitcast(BF16)
        bdst = gl[:, 0:2, :].rearrange("p a (b c) -> p (a b) c", c=128)

        win_v = win_d.rearrange("(k p) c -> p k c", p=128)
        wout_v = wout_d.rearrange("(k p) c -> p k c", p=128)
        w1_v = w1_d.rearrange("(k p) c -> p k c", p=128)
        w2_v = w2_d.rearrange("(k p) c -> p k c", p=128)

        def emit_program(S, dry, order):
            B_ring = [Buf(f"ring{i}") for i in range(NRING)]
            B_X = [[Buf(f"x{b}_{m}") for m in range(4)] for b in range(2)]
            B_z = [Buf(f"z{j}") for j in range(3)]
            B_zg = [[Buf(f"zg{g}_{h}") for h in range(2)] for g in range(2)]
            B_hTa, B_hTb = Buf("hTa"), Buf("hTb")
            B_ybf = [Buf(f"ybf{j}") for j in range(12)]
            B_gl = [Buf(f"gl{j}") for j in range(8)]
            B_gbc = [Buf(f"gbc{i}") for i in range(3)]
            B_hn = [Buf(f"hn{i}") for i in range(2)]
            B_junk = Buf("junk")
            B_T = [{n: Buf(f"t{s}_{n}") for n in TNAMES} for s in range(NSLOT)]
            B_TA = [{n: Buf(f"ta{s}_{n}") for n in ("xh", "acc", "tr")} for s in range(NA_SLOT)]
            B_xrb = [Buf(f"xrb{s}") for s in range(NSLOT)]
            B_ysq = [Buf(f"ysq{i}") for i in range(4)]
            B_rt = [Buf(f"rt{i}") for i in range(3)]
            B_BD = [Buf("bda"), Buf("bdx")]
            B_pv, B_coef, B_nba, B_nbx, B_tmp8 = Buf("pv"), Buf("coef"), Buf("nba"), Buf("nbx"), Buf("tmp8")
            B_histA = [Buf(f"histA{j}") for j in range(4)]
            B_histB = [Buf(f"histB{j}") for j in range(8)]
            B_hst = [Buf(f"hst{j}") for j in range(8)]
            B_ident, B_identf, B_ones = Buf("ident"), Buf("identf"), Buf("ones")
            B_ss0, B_rs0, B_ss1, B_rs1 = Buf("ss0"), Buf("rs0"), Buf("ss1"), Buf("rs1")
            B_ss2 = [Buf(f"ss2_{m}") for m in range(4)]
            B_rs2 = [Buf(f"rs2_{m}") for m in range(4)]
            B_ssr, B_rsAB = Buf("ssr"), Buf("rsAB")
            B_bank = [Buf(f"bank{i}") for i in range(8)]
            B_ws = [Buf(f"ws{i}") for i in range(NBLK)]
            B_wsp = [[Buf(f"ws{i}_{k}") for k in range(3)] for i in range(NBLK)]
            B_bdst_a = Buf("bdst_a")
            B_ginv = Buf("ginv")

            for i in range(4):
                S.dma_sem(f"setup{i}")
            for i in range(2):
                S.dma_sem(f"bd{i}")
            for i in range(NBLK):
                S.dma_sem(f"cast{i}")
            for i in range(NRING):
                S.dma_sem(f"ring{i}")
            for b in range(2):
                for m in range(4):
                    S.dma_sem(f"xl{b}_{m}")
                    S.dma_sem(f"xs{b}_{m}")

            S.op("sp", f_dma(pv[:], pv_d), writes=[B_pv], dma="setup3")
            for i, g_d in enumerate((gmix_d, gmlp_d, gfin_d)):
                S.op("sp", f_dma(gbc[i][:], g_d.partition_broadcast(128)), writes=[B_gbc[i]], dma=f"setup{i}")
            S.op("pool", f_memset(identf[:], 0.0), writes=[B_identf])
            S.op("pool", lambda e: e.affine_select(out=identf[:], in_=identf[:], pattern=[[-1, 128]],
                                                  compare_op=ALU.not_equal, fill=1.0, base=0, channel_multiplier=1),
                 reads=[B_identf], writes=[B_identf])
            S.op("dve", f_copy(ident[:], identf[:]), reads=[B_identf], writes=[B_ident])
            S.op("pool", f_memset(ones[:], 1.0), writes=[B_ones])
            S.op("pool", f_memset(histA[:], 0.0), writes=B_histA)
            S.op("pool", f_memset(histB[:], 0.0), writes=B_histB)
            S.op("pool", f_memset(hstate[:], 0.0), writes=B_hst)
            B_bdst = [B_gl[0], B_gl[1]]
            for gi, w_d in enumerate((wa_d, wx_d)):
                S.op("pool", f_memset(bdst, 0.0), writes=B_bdst + [B_bdst_a])
                wv = w_d.rearrange("(j h) i k -> h i j k", h=2)
                S.op("sp", f_dma(bdst[0:64, :, 0:64], wv[0]), reads=B_bdst, writes=[B_bdst_a], dma=f"bd{gi}")
                S.op("sp", f_dma(bdst[64:128, :, 64:128], wv[1]), writes=B_bdst, dma=f"bd{gi}")
                S.op("dve", f_copy(BD[gi][:], bdst), reads=B_bdst + [B_bdst_a], writes=[B_BD[gi]])
            S.op("act", f_act(tmp8[:], pv[:, PV_LAM:PV_LAM + 8], AF.Exp, scale=-1.0), reads=[B_pv], writes=[B_tmp8])
            S.op("act", f_act(tmp8[:], tmp8[:], AF.Ln, bias=1.0), reads=[B_tmp8], writes=[B_tmp8])
            S.op("dve", f_ts(coef[:], tmp8[:], -8.0, None, ALU.mult), reads=[B_tmp8], writes=[B_coef])
            S.op("dve", f_ts(coef2[:], tmp8[:], -16.0, None, ALU.mult), reads=[B_tmp8], writes=[B_coef])
            S.op("dve", _meta(lambda e: e.reciprocal(out=ginv[:], in_=pv[:, PV_GC:PV_GC + 12]), "ts", 12), reads=[B_pv], writes=[B_ginv])
            S.op("dve", f_ts(ginv[:], ginv[:], 1e30, -1e30, ALU.min, ALU.max), reads=[B_ginv], writes=[B_ginv])
            S.op("dve", f_ts(nba[:], pv[:, PV_BA:PV_BA + 8], -1.0, None, ALU.mult), reads=[B_pv], writes=[B_nba])
            S.op("dve", f_ts(nbx[:], pv[:, PV_BX:PV_BX + 8], -1.0, None, ALU.mult), reads=[B_pv], writes=[B_nbx])

            def cast_block(b):
                _cast_block(b)

            def _thr(b):
                return [B_ws[b - CAST_DEPTH]] if b >= CAST_DEPTH else []

            def _cast_block(b):
                dst = ws_d[b]
                if b < 4:
                    c0 = ORDER28[4 * b] * 128
                    S.op("pool", f_dma(dst.rearrange("p (k c) -> p k c", k=8), win_v[:, :, c0:c0 + 512]),
                         reads=_thr(b), writes=[B_ws[b]], dma=f"cast{b}")
                elif b < 7:
                    dv = dst.rearrange("p (k c) -> p k c", k=8)
                    for i in range(4):
                        c0 = ORDER28[4 * b + i] * 128
                        S.op("pool", f_dma(dv[:, :, i * 128:(i + 1) * 128], win_v[:, :, c0:c0 + 128]),
                             reads=(_thr(b) if i == 0 else []), writes=([B_ws[b]] if i == 3 else [B_wsp[b][i]]), dma=f"cast{b}")
                elif b < 10:
                    k0 = 4 * (b - 7)
                    S.op("pool", f_dma(dst.rearrange("p (k c) -> p k c", k=4), wout_v[:, k0:k0 + 4, :]),
                         reads=_thr(b), writes=[B_ws[b]], dma=f"cast{b}")
                elif (b - 10) % 2 == 0:
                    j = (b - 10) // 2
                    S.op("pool", f_dma(dst.rearrange("p (k c) -> p k c", k=8), w1_v[:, :, j * 512:(j + 1) * 512]),
                         reads=_thr(b), writes=[B_ws[b]], dma=f"cast{b}")
                else:
                    j = (b - 11) // 2
                    if W2_FG:
                        fg, n = j // 2, j % 2
                        S.op("pool", f_dma(dst.rearrange("p (k c) -> p k c", k=8), w2_v[:, 8 * fg:8 * fg + 8, n * 512:(n + 1) * 512]),
                             reads=_thr(b), writes=[B_ws[b]], dma=f"cast{b}")
                    else:
                        S.op("pool", f_dma(dst.rearrange("p (k c) -> p k c", k=4), w2_v[:, 4 * j:4 * j + 4, :]),
                             reads=_thr(b), writes=[B_ws[b]], dma=f"cast{b}")

            cast_next = [0]

            def ensure_cast(upto):
                while cast_next[0] < min(upto + 1, NBLK):
                    cast_block(cast_next[0])
                    cast_next[0] += 1

            first_direct = set()

            def emit_load(b, slot):
                if b < 4 and b not in first_direct:
                    first_direct.add(b)
                    c0 = ORDER28[4 * b] * 128
                    S.op("pool", f_dma(ring[slot][:].rearrange("p (k c) -> p k c", k=8), win_v[:, :, c0:c0 + 512]),
                         writes=[B_ring[slot]], dma=f"ring{slot}")
                    return
                ensure_cast(b + CAST_AHEAD)
                S.op("sp", f_dma(ring[slot][:], ws_d[b]), reads=[B_ws[b]] + (B_wsp[b] if 4 <= b < 7 else []),
                     writes=[B_ring[slot]], dma=f"ring{slot}")

            RM = RingMgr(NRING, dry, order, emit_load)

            def load_x_tile(s, m):
                xb = s % 2
                r0 = s * T + m * 128
                S.op("sp", f_dma(X[xb][m][:], x_d[r0:r0 + 128, :]), writes=[B_X[xb][m]], dma=f"xl{xb}_{m}")

            rr = {"win": 0, "wo": 0, "mlp": 0, "ysq": 0, "rt": 0}

            def rot(name, choices):
                i = choices[rr[name] % len(choices)]
                rr[name] += 1
                return i

            def norm_stats(xb, m, ss, B_ss):
                S.op("act", f_act(junk[:], X[xb][m][:], AF.Square, accum=ss[:, m:m + 1]),
                     reads=[B_X[xb][m]], writes=[B_junk, B_ss])

            def norm_and_transpose(xb, ss, rs, B_ss, B_rs, gi, hT, B_hT, stats_done=False):
                for m in range(4):
                    if not stats_done:
                        norm_stats(xb, m, ss, B_ss)
                S.op("act", f_act(ss[:], ss[:], AF.Ln, bias=EPS, scale=1.0 / D), reads=[B_ss], writes=[B_ss])
                S.op("act", f_act(rs[:], ss[:], AF.Exp, scale=-0.5), reads=[B_ss], writes=[B_rs])
                yield

                def nstt(m):
                    h = m % 2
                    S.op("dve", f_stt(hn[h][:], X[xb][m][:], rs[:, m:m + 1], gbc[gi][:], ALU.mult, ALU.mult),
                         reads=[B_X[xb][m], B_rs, B_gbc[gi]], writes=[B_hn[h]])

                nstt(0)
                yield
                for m in range(4):
                    h = m % 2
                    if m + 1 < 4:
                        nstt(m + 1)
                        yield
                    for k in range(8):
                        S.op("pe", f_tr(psT[:, k * 128:(k + 1) * 128], hn[h][:, k * 128:(k + 1) * 128], ident[:]),
                             reads=[B_hn[h], B_ident], writes=[B_bank[TB]], signal=(k == 7))
                    S.op("act", f_act(hT[:, :, m * 128:(m + 1) * 128], psT.rearrange("p (k c) -> p k c", k=8), AF.Copy),
                         reads=[B_bank[TB]], writes=[B_hT])
                    yield

            ss_pending = []

            def ss_matmuls(yi, jj, keep=2):
                ss_pending.append((yi, jj))
                ss_flush(keep)

            def ss_flush(keep):
                while len(ss_pending) > keep:
                    yi, jj = ss_pending.pop(0)
                    for m in range(4):
                        c = m * 12 + jj
                        S.op("pe", f_mm(banks[TB][:, c:c + 1], ysq[yi][:, m * 128:(m + 1) * 128], ones[:, 0:1], True, True),
                             reads=[B_ysq[yi], B_ones], writes=[B_bank[TB]], signal=(m == 3))

            def pe_(s):
                return "dve" if s < POOL_FREE_CHUNKS else "pool"

            def mix_thread(s):
                xb = s % 2
                yield from norm_and_transpose(xb, ss0, rs0, B_ss0, B_rs0, 0, hTa, B_hTa)

                cur = {"b": -1, "u": None}

                def win_tile(pos):
                    b = pos // 4
                    if b != cur["b"]:
                        if cur["u"] is not None:
                            RM.release(cur["u"])
                        cur["u"] = RM.use(blk_win(b))
                        cur["b"] = b
                    slot = cur["u"][1]
                    i = pos % 4
                    bi = rot("win", WIN_BANKS)
                    wv = ring[slot][:].rearrange("p (k c) -> p k c", k=8)
                    for k in range(8):
                        S.op("pe", f_mm(banks[bi][:], wv[:, k, i * 128:(i + 1) * 128], hTa[:, k, :], k == 0, k == 7),
                             reads=[B_ring[slot], B_hTa], writes=[B_bank[bi]], signal=(k == 7))
                    return bi

                for j in range(8):
                    bi = win_tile(j)
                    S.op("act", f_act(gl[:, j, :], banks[bi][:], AF.Gelu_apprx_tanh), reads=[B_bank[bi]], writes=[B_gl[j]])
                    yield

                def stageA(j):
                    sl = j % NSLOT
                    t, Bt = TT[sl], B_T[sl]
                    S.op(pe_(s), f_copy(t["xh"][:, 0:3], histB[:, j, :]), reads=[B_histB[j]], writes=[Bt["xh"]])
                    bi = win_tile(8 + j)
                    if XREV_ENG:
                        S.op("dve", f_copy(t["xh"][:, 3:515], banks[bi][:]), reads=[B_bank[bi]], writes=[Bt["xh"]])
                    else:
                        S.op("act", f_act(t["xh"][:, 3:515], banks[bi][:], AF.Copy), reads=[B_bank[bi]], writes=[Bt["xh"]])
                    S.op(pe_(s), f_copy(histB[:, j, :], t["xh"][:, 512:515]), reads=[Bt["xh"]], writes=[B_histB[j]])
                    rw = lambda k: pv[:, PV_RW + k * 8 + j:PV_RW + k * 8 + j + 1]
                    S.op("dve", f_ts(t["acc"][:, 0:T], t["xh"][:, 3:515], rw(3), pv[:, PV_RB + j:PV_RB + j + 1], ALU.mult, ALU.add),
                         reads=[Bt["xh"], B_pv], writes=[Bt["acc"]])
                    for k in (2, 1, 0):
                        S.op("dve", f_stt(t["acc"][:, 0:T], t["xh"][:, k:k + T], rw(k), t["acc"][:, 0:T], ALU.mult, ALU.add),
                             reads=[Bt["xh"], Bt["acc"], B_pv], writes=[Bt["acc"]])
                    if XRB_ENG:
                        S.op("pool", f_copy(xrb[sl][:], t["acc"][:, 0:T]), reads=[Bt["acc"]], writes=[B_xrb[sl]])
                    elif s < FILL_DVE:
                        S.op("dve", f_copy(xrb[sl][:], t["acc"][:, 0:T]), reads=[Bt["acc"]], writes=[B_xrb[sl]])
                    else:
                        S.op("act", f_act(xrb[sl][:], t["acc"][:, 0:T], AF.Copy), reads=[Bt["acc"]], writes=[B_xrb[sl]])

                def stageA2(j):
                    sl = j % NSLOT
                    S.op("pe", f_mm(banks[2][:], BD[0][:, j, :], xrb[sl][:], True, True), reads=[B_BD[0], B_xrb[sl]], writes=[B_bank[2]])
                    S.op("pe", f_mm(banks[3][:], BD[1][:, j, :], xrb[sl][:], True, True), reads=[B_BD[1], B_xrb[sl]], writes=[B_bank[3]])

                def stageB(j):
                    sl = j % NSLOT
                    t, Bt = TT[sl], B_T[sl]
                    tr, ti, a = t["tr"][:, 0:T], t["ti"][:, 0:T], t["a"][:, 0:T]
                    S.op("act", f_act(tr, banks[2][:], AF.Exp, bias=nba[:, j:j + 1], scale=-1.0), reads=[B_bank[2], B_nba], writes=[Bt["tr"]])
                    S.op("act", f_act(ti, banks[3][:], AF.Exp, bias=nbx[:, j:j + 1], scale=-1.0), reads=[B_bank[3], B_nbx], writes=[Bt["ti"]])
                    S.op("act", f_act(tr, tr, AF.Ln, bias=1.0), reads=[Bt["tr"]], writes=[Bt["tr"]])
                    S.op("act", f_act(ti, ti, AF.Ln, bias=1.0), reads=[Bt["ti"]], writes=[Bt["ti"]])
                    S.op("act", f_act(tr, tr, AF.Exp, scale=-1.0), reads=[Bt["tr"]], writes=[Bt["tr"]])
                    S.op("act", f_act(ti, ti, AF.Exp, scale=-1.0), reads=[Bt["ti"]], writes=[Bt["ti"]])
                    S.op("act", f_act(a, tr, AF.Exp, scale=coef[:, j:j + 1]), reads=[Bt["tr"], B_coef], writes=[Bt["a"]])
                    S.op(pe_(s), f_tt(ti, ti, t["acc"][:, 0:T], ALU.mult), reads=[Bt["ti"], Bt["acc"]], writes=[Bt["ti"]])
                    S.op("act", f_act(tr, tr, AF.Exp, scale=coef2[:, j:j + 1]), reads=[Bt["tr"], B_coef], writes=[Bt["tr"]])
                    S.op("act", f_act(tr, tr, AF.Ln, bias=1.0, scale=-1.0), reads=[Bt["tr"]], writes=[Bt["tr"]])
                    S.op("act", f_act(tr, tr, AF.Exp, scale=0.5), reads=[Bt["tr"]], writes=[Bt["tr"]])

                def stageC(j):
                    sl = j % NSLOT
                    t, Bt = TT[sl], B_T[sl]
                    tr, ti, a = t["tr"][:, 0:T], t["ti"][:, 0:T], t["a"][:, 0:T]
                    h = t["xh"][:, 0:T]
                    S.op("dve", f_tt(ti, ti, tr, ALU.mult), reads=[Bt["ti"], Bt["tr"]], writes=[Bt["ti"]])
                    S.op("dve", f_scan(h, a, ti, hstate[:, j:j + 1]), reads=[Bt["a"], Bt["ti"], B_hst[j]], writes=[Bt["xh"]])
                    S.op(pe_(s), f_copy(hstate[:, j:j + 1], t["xh"][:, T - 1:T]), reads=[Bt["xh"]], writes=[B_hst[j]])
                    S.op("dve", f_stt(ybf[:, 4 + j, :], h, pv[:, PV_GR + j:PV_GR + j + 1], gl[:, j, :], ALU.mult, ALU.mult),
                         reads=[Bt["xh"], B_gl[j], B_pv], writes=[B_ybf[4 + j]])
                    yi = rot("ysq", (0, 1, 2, 3))
                    S.op("act", f_act(ysq[yi][:], ybf[:, 4 + j, :], AF.Square, scale=ginv[:, 4 + j:5 + j]),
                         reads=[B_ybf[4 + j], B_ginv], writes=[B_ysq[yi]])
                    ss_matmuls(yi, 4 + j)

                for step in range(8 + 2):
                    if step < 8:
                        stageA(step)
                        yield
                    if 1 <= step < 9:
                        stageB(step - 1)
                        yield
                    if step < 8:
                        stageA2(step)
                        yield
                    if 2 <= step < 10:
                        stageC(step - 2)
                        yield

                for j in range(4):
                    if NA_SLOT:
                        sl = j % NA_SLOT
                        t, Bt = TA[sl], B_TA[sl]
                    else:
                        sl = j % NSLOT
                        t, Bt = TT[sl], B_T[sl]
                    gc, cvh, acc = t["tr"][:, 0:T], t["xh"], t["acc"][:, 0:T]
                    S.op(pe_(s), f_copy(cvh[:, 0:2], histA[:, j, :]), reads=[B_histA[j]], writes=[Bt["xh"]])
                    bi = win_tile(16 + 3 * j)
                    S.op("act", f_act(gc, banks[bi][:], AF.Copy), reads=[B_bank[bi]], writes=[Bt["tr"]])
                    bi = win_tile(16 + 3 * j + 1)
                    S.op("dve", f_tt(cvh[:, 2:514], banks[bi][:], gc, ALU.mult), reads=[B_bank[bi], Bt["tr"]], writes=[Bt["xh"]])
                    S.op(pe_(s), f_copy(histA[:, j, :], cvh[:, 512:514]), reads=[Bt["xh"]], writes=[B_histA[j]])
                    yield
                    cw = lambda k: pv[:, PV_CW + k * 4 + j:PV_CW + k * 4 + j + 1]
                    S.op("dve", f_ts(acc, cvh[:, 2:514], cw(2), None, ALU.mult), reads=[Bt["xh"], B_pv], writes=[Bt["acc"]])
                    for k in (1, 0):
                        S.op("dve", f_stt(acc, cvh[:, k:k + T], cw(k), acc, ALU.mult, ALU.add),
                             reads=[Bt["xh"], Bt["acc"], B_pv], writes=[Bt["acc"]])
                    yield
                    bi = win_tile(16 + 3 * j + 2)
                    S.op("dve", f_stt(ybf[:, j, :], banks[bi][:], pv[:, PV_GC + j:PV_GC + j + 1], acc, ALU.mult, ALU.mult),
                         reads=[B_bank[bi], Bt["acc"], B_pv], writes=[B_ybf[j]])
                    yi = rot("ysq", (0, 1, 2, 3))
                    S.op("act", f_act(ysq[yi][:], ybf[:, j, :], AF.Square, scale=ginv[:, j:j + 1]),
                         reads=[B_ybf[j], B_ginv], writes=[B_ysq[yi]])
                    ss_matmuls(yi, j)
                    yield
                RM.release(cur["u"])
                ss_flush(0)

                ssv = banks[TB][:, 0:48].rearrange("p (m j) -> p m j", j=12)
                S.op("dve", lambda e: e.tensor_reduce(out=ssr[:, 0:4], in_=ssv[:, :, 0:4], axis=mybir.AxisListType.X, op=ALU.add),
                     reads=[B_bank[TB]], writes=[B_ssr])
                S.op("dve", lambda e: e.tensor_reduce(out=ssr[:, 4:8], in_=ssv[:, :, 4:12], axis=mybir.AxisListType.X, op=ALU.add),
                     reads=[B_bank[TB]], writes=[B_ssr])
                S.op("act", f_act(ssr[:, 0:4], ssr[:, 0:4], AF.Ln, bias=EPS, scale=1.0 / 512), reads=[B_ssr], writes=[B_ssr])
                S.op("act", f_act(ssr[:, 4:8], ssr[:, 4:8], AF.Ln, bias=EPS, scale=1.0 / 1024), reads=[B_ssr], writes=[B_ssr])
                S.op("act", f_act(rsAB[:], ssr[:], AF.Exp, scale=-0.5), reads=[B_ssr], writes=[B_rsAB])
                wo = [RM.use(blk_wout(i)) for i in range(3)]
                for m in range(4):
                    for n in range(2):
                        pa = rot("wo", (0, 2))
                        pb = pa + 1
                        for k in range(12):
                            slot = wo[k // 4][1]
                            wv = ring[slot][:].rearrange("p (k c) -> p k c", k=4)
                            bi = pa if k < 4 else pb
                            S.op("pe", f_mm(banks[bi][:], ybf[:, k, m * 128:(m + 1) * 128], wv[:, k % 4, n * 512:(n + 1) * 512],
                                            k in (0, 4), k in (3, 11)),
                                 reads=[B_ring[slot], B_ybf[k]], writes=[B_bank[bi]], signal=(k in (3, 11)))
                        xs = X[xb][m][:, n * 512:(n + 1) * 512]
                        S.op("dve", f_stt(xs, banks[pa][:], rsAB[:, m:m + 1], xs, ALU.mult, ALU.add),
                             reads=[B_bank[pa], B_rsAB, B_X[xb][m]], writes=[B_X[xb][m]])
                        S.op("dve", f_stt(xs, banks[pb][:], rsAB[:, 4 + m:5 + m], xs, ALU.mult, ALU.add),
                             reads=[B_bank[pb], B_rsAB, B_X[xb][m]], writes=[B_X[xb][m]])
                        if n == 1:
                            norm_stats(xb, m, ss1, B_ss1)
                        yield
                for u in wo:
                    RM.release(u)
                yield from norm_and_transpose(xb, ss1, rs1, B_ss1, B_rs1, 1, hTb, B_hTb, stats_done=True)

            store_tickets = []

            def mlp_thread(s):
                xb = s % 2
                MB = MLP_BANKS

                def w1_block(j):
                    u = RM.use(blk_w1(j))
                    wv = ring[u[1]][:].rearrange("p (k c) -> p k c", k=8)
                    zi = j % 3
                    for f in range(4):
                        bi = rot("mlp", MB)
                        for k in range(8):
                            S.op("pe", f_mm(banks[bi][:], wv[:, k, f * 128:(f + 1) * 128], hTb[:, k, :], k == 0, k == 7),
                                 reads=[B_ring[u[1]], B_hTb], writes=[B_bank[bi]], signal=(k == 7))
                        ri = rot("rt", (0, 1, 2))
                        S.op("act", f_act(rtt[ri][:], banks[bi][:], AF.Relu), reads=[B_bank[bi]], writes=[B_rt[ri]])
                        if W2_FG:
                            g, h = (j // 2) % 2, j % 2
                            if s < POOL_FREE_CHUNKS:
                                S.op("act", f_act(zb[g][:, 4 * h + f, :], rtt[ri][:], AF.Square), reads=[B_rt[ri]], writes=[B_zg[g][h]])
                            else:
                                S.op("pool", f_tt(zb[g][:, 4 * h + f, :], rtt[ri][:], rtt[ri][:], ALU.mult), reads=[B_rt[ri]], writes=[B_zg[g][h]])
                        else:
                            S.op("pool", f_tt(zb[zi][:, f, :], rtt[ri][:], rtt[ri][:], ALU.mult), reads=[B_rt[ri]], writes=[B_z[zi]])
                        yield
                    RM.release(u)

                def w2_block(j):
                    u = RM.use(blk_w2(j))
                    wv = ring[u[1]][:].rearrange("p (k c) -> p k c", k=4)
                    zi = j % 3
                    for m in range(4):
                        for n in range(2):
                            bi = rot("mlp", MB)
                            for f in range(4):
                                S.op("pe", f_mm(banks[bi][:], zb[zi][:, f, m * 128:(m + 1) * 128], wv[:, f, n * 512:(n + 1) * 512],
                                                f == 0, f == 3),
                                     reads=[B_ring[u[1]], B_z[zi]], writes=[B_bank[bi]], signal=(f == 3))
                            xs = X[xb][m][:, n * 512:(n + 1) * 512]
                            S.op("dve", f_tt(xs, banks[bi][:], xs, ALU.add), reads=[B_bank[bi], B_X[xb][m]], writes=[B_X[xb][m]])
                            yield
                    RM.release(u)

                def w2_fg(fg, n):
                    u = RM.use(blk_w2(2 * fg + n))
                    wv = ring[u[1]][:].rearrange("p (k c) -> p k c", k=8)
                    g = fg % 2
                    for m in range(4):
                        bi = rot("mlp", MB)
                        for f in range(8):
                            S.op("pe", f_mm(banks[bi][:], zb[g][:, f, m * 128:(m + 1) * 128], wv[:, f, :], f == 0, f == 7),
                                 reads=[B_ring[u[1]], B_zg[g][0], B_zg[g][1]], writes=[B_bank[bi]], signal=(f == 7))
                        xs = X[xb][m][:, n * 512:(n + 1) * 512]
                        S.op("dve", f_tt(xs, banks[bi][:], xs, ALU.add), reads=[B_bank[bi], B_X[xb][m]], writes=[B_X[xb][m]])
                        yield
                    RM.release(u)

                if W2_FG:
                    if MLP_SEQ == 1:
                        seq = [("w1", 0), ("w1", 1), ("w2", 0, 0), ("w1", 2), ("w2", 0, 1), ("w1", 3), ("w2", 1, 0), ("w1", 4),
                               ("w2", 1, 1), ("w1", 5), ("w2", 2, 0), ("w1", 6), ("w2", 2, 1), ("w1", 7), ("w2", 3, 0), ("w2", 3, 1)]
                    elif MLP_SEQ == 2:
                        seq = [("w1", 0), ("w1", 1), ("w1", 2), ("w1", 3), ("w2", 0, 0), ("w2", 0, 1), ("w1", 4), ("w1", 5),
                               ("w2", 1, 0), ("w2", 1, 1), ("w1", 6), ("w1", 7), ("w2", 2, 0), ("w2", 2, 1), ("w2", 3, 0), ("w2", 3, 1)]
                    else:
                        seq = [("w1", 0), ("w1", 1), ("w1", 2), ("w2", 0, 0), ("w1", 3), ("w2", 0, 1), ("w1", 4), ("w2", 1, 0),
                               ("w1", 5), ("w2", 1, 1), ("w1", 6), ("w2", 2, 0), ("w1", 7), ("w2", 2, 1), ("w2", 3, 0), ("w2", 3, 1)]
                    for it in seq:
                        if it[0] == "w1":
                            yield from w1_block(it[1])
                        else:
                            yield from w2_fg(it[1], it[2])
                else:
                    yield from w1_block(0)
                    for j in range(1, 8):
                        yield from w1_block(j)
                        yield from w2_block(j - 1)
                    yield from w2_block(7)
                for m in range(4):
                    S.op("act", f_act(junk[:], X[xb][m][:], AF.Square, accum=ss2[:, m:m + 1]),
                         reads=[B_X[xb][m]], writes=[B_junk, B_ss2[m]])
                    S.op("act", f_act(ss2[:, m:m + 1], ss2[:, m:m + 1], AF.Ln, bias=EPS, scale=1.0 / D), reads=[B_ss2[m]], writes=[B_ss2[m]])
                    S.op("act", f_act(rs2[:, m:m + 1], ss2[:, m:m + 1], AF.Exp, scale=-0.5), reads=[B_ss2[m]], writes=[B_rs2[m]])
                    S.op("dve", f_stt(X[xb][m][:], X[xb][m][:], rs2[:, m:m + 1], gbc[2][:], ALU.mult, ALU.mult),
                         reads=[B_X[xb][m], B_rs2[m], B_gbc[2]], writes=[B_X[xb][m]])
                    r0 = s * T + m * 128
                    tk = S.op("sp", f_dma(y_d[r0:r0 + 128, :], X[xb][m][:]), reads=[B_X[xb][m]], writes=[], dma=f"xs{xb}_{m}")
                    store_tickets.append(tk)
                    if s + 2 < NCH:
                        load_x_tile(s + 2, m)
                    yield

            def drive_rate(ga, gb, rate):
                acc = 0.0
                a_alive, b_alive = ga is not None, gb is not None
                while a_alive:
                    try:
                        S.cur_thread = 0
                        next(ga)
                    except StopIteration:
                        a_alive = False
                        break
                    acc += rate
                    while b_alive and acc >= 1.0:
                        acc -= 1.0
                        try:
                            S.cur_thread = 1
                            next(gb)
                        except StopIteration:
                            b_alive = False
                while b_alive:
                    try:
                        S.cur_thread = 1
                        next(gb)
                    except StopIteration:
                        b_alive = False

            def drive(ga, gb, na, nb):
                if MLP_RATE > 0:
                    return drive_rate(ga, gb, MLP_RATE)
                alive = [ga is not None, gb is not None]
                gens = [ga, gb]
                cnt = [na, nb]
                while alive[0] or alive[1]:
                    for i in range(2):
                        if not alive[i]:
                            continue
                        for _ in range(cnt[i] if alive[1 - i] else 1000000):
                            try:
                                S.cur_thread = i
                                next(gens[i])
                            except StopIteration:
                                alive[i] = False
                                break

            for m in range(4):
                load_x_tile(0, m)
            if NCH > 1:
                for m in range(4):
                    load_x_tile(1, m)
            drive(mix_thread(0), None, 1, 1)
            for s in range(NCH):
                drive(mix_thread(s + 1) if s + 1 < NCH else None, mlp_thread(s), MIX_STEPS, MLP_STEPS)
            S.wait_all("sp", store_tickets)

        order = []
        emit_program(GraphSched(), True, order)
        G = GraphSched()
        emit_program(G, False, order)
        S = G.schedule()
        if os.environ.get("MK_VERBOSE"):
            print("list-schedule simulated time (us):", G.sim_time, "ops", len(G.ops), {k: round(v) for k, v in G.sim_busy.items()})

        with contextlib.ExitStack() as es2:
            sems = {n: es2.enter_context(nc.semaphore(n)) for n in S.sem_names}
            block = es2.enter_context(nc.Block())

            @block.sync
            def _(e):
                S.emit("sp", e, sems)

            @block.scalar
            def _(e):
                S.emit("act", e, sems)

            @block.vector
            def _(e):
                S.emit("dve", e, sems)

            @block.gpsimd
            def _(e):
                S.emit("pool", e, sems)

            @block.tensor
            def _(e):
                S.emit("pe", e, sems)
    return nc


def _pack_params(conv_w, rnn_conv_w, rnn_conv_b, b_a, b_x, lru_lambda, g_norm_conv, g_norm_rnn):
    def cols(v):
        return np.ascontiguousarray(v.reshape(-1, 128).T)
    parts = []
    parts.append(np.concatenate([cols(conv_w[0, k]) for k in range(3)], axis=1))
    parts.append(np.concatenate([cols(rnn_conv_w[0, k]) for k in range(4)], axis=1))
    for v in (rnn_conv_b[0], b_a[0], b_x[0], lru_lambda[0], g_norm_conv[0], g_norm_rnn[0]):
        parts.append(cols(v))
    pv = np.ascontiguousarray(np.concatenate(parts, axis=1).astype(np.float32))
    assert pv.shape == (128, NPV), pv.shape
    return pv


def kernel(x, norm_mix_g, w_in, conv_w, rnn_conv_w, rnn_conv_b, w_a, b_a, w_x, b_x, lru_lambda,
           g_norm_conv, g_norm_rnn, w_out, norm_mlp_g, w_mlp_in, w_mlp_out, final_norm_g):
    nch = int(os.environ.get("MK_NCH", NCH_FULL))
    f = lambda a: np.ascontiguousarray(np.asarray(a, dtype=np.float32))
    x = f(x)
    pv = _pack_params(f(conv_w), f(rnn_conv_w), f(rnn_conv_b), f(b_a), f(b_x), f(lru_lambda),
                      f(g_norm_conv), f(g_norm_rnn))
    shared = {
        "w_in": f(w_in)[0], "w_out": f(w_out)[0], "w1": f(w_mlp_in)[0], "w2": f(w_mlp_out)[0],
        "w_a": f(w_a)[0], "w_x": f(w_x)[0], "pv": pv,
        "g_mix": f(norm_mix_g)[0], "g_mlp": f(norm_mlp_g)[0], "g_fin": f(final_norm_g),
    }
    nc = build(nch)
    ncores = x.shape[0]
    in_maps = [dict(shared, x=x[c]) for c in range(ncores)]
    res = run_bass_kernel_spmd(nc, in_maps, core_ids=list(range(ncores)))
    out = np.stack([np.asarray(res.results[c]["y"], dtype=np.float32) for c in range(ncores)], axis=0)
    if int(os.environ.get("MK_DBG", "0")):
        kernel.dbg = {k: np.asarray(v) for k, v in res.results[0].items() if k.startswith("d_")}
    return out
```
